# Optimizing a Trainium2 kernel written in Bass

```python
import jax, jax.numpy as jnp
from jax import lax
import numpy as np

D_MODEL = 1024
BATCH = 8
SEQ = 2048
DEPTH = 2

N_MIXERS = 2
SB_HEADS = 16
SB_HEAD_DIM = D_MODEL // SB_HEADS
Q_BLOCK = 128
D_RNN = (D_MODEL * 5) // 4
LRU_BLOCKS = 10
LRU_BLOCK_W = D_RNN // LRU_BLOCKS
CONV_W = 4
LRU_C = 8.0
D_FF = ((8 * D_MODEL // 3 + 127) // 128) * 128
N_SUB = 3
MACARON_W = 0.5
EPS = 1e-6
N_SB_LAYERS = (DEPTH + 1) // 2
N_LRU_LAYERS = DEPTH // 2

kernel_name = "hybrid_stickbreak_rglru_macaron_adaln"


def rmsnorm(x, g):
    xf = x.astype(jnp.float32)
    inv = lax.rsqrt(jnp.mean(xf * xf, axis=-1, keepdims=True) + EPS)
    return (xf * inv).astype(x.dtype) * g


def sublayer_input(x, g, mod_j):
    shift, scale, gate = mod_j[:, 0], mod_j[:, 1], mod_j[:, 2]
    h = rmsnorm(x, g) * (1 + scale[:, None, :]) + shift[:, None, :]
    return h, (1 + gate)[:, None, :]


def swiglu(h, w_gu, w_down):
    g, u = jnp.split(h @ w_gu, 2, axis=-1)
    return (jax.nn.silu(g) * u) @ w_down


def stick_breaking_attention(h, w_qkv, w_o):
    B, S, _ = h.shape
    qkv = (h @ w_qkv).reshape(B, S, 3, SB_HEADS, SB_HEAD_DIM)
    qkv = jnp.transpose(qkv, (2, 0, 3, 1, 4)).astype(jnp.float32)
    q = qkv[0] * (SB_HEAD_DIM ** -0.5)
    k, v = qkv[1], qkv[2]
    outs = []
    for blk in range(S // Q_BLOCK):
        t0 = blk * Q_BLOCK
        L = t0 + Q_BLOCK
        z = jnp.einsum('bhqd,bhkd->bhqk', q[:, :, t0:L], k[:, :, :L])
        t_idx = t0 + jnp.arange(Q_BLOCK)
        s_idx = jnp.arange(L)
        mask = s_idx[None, :] < t_idx[:, None]
        log_keep = jnp.where(mask, jax.nn.log_sigmoid(-z), 0.0)
        later = lax.cumsum(log_keep, axis=3, reverse=True) - log_keep
        w = jnp.where(mask, jnp.exp(jax.nn.log_sigmoid(z) + later), 0.0)
        outs.append(jnp.einsum('bhqk,bhkd->bhqd', w, v[:, :, :L]))
    o = jnp.concatenate(outs, axis=2)
    o = jnp.transpose(o, (0, 2, 1, 3)).reshape(B, S, D_MODEL).astype(h.dtype)
    return o @ w_o


def _lin_rec_combine(left, right):
    a1, b1 = left
    a2, b2 = right
    return a1 * a2, a2 * b1 + b2


def rglru_block(h, w_in, conv_w, conv_b, w_r, b_r, w_i, b_i, lam, w_out):
    B, S, _ = h.shape
    gate, xb = jnp.split(h @ w_in, 2, axis=-1)
    xp = jnp.pad(xb, ((0, 0), (CONV_W - 1, 0), (0, 0)))
    xc = conv_b + xp[:, 0:S] * conv_w[0]
    for tap in range(1, CONV_W):
        xc = xc + xp[:, tap:tap + S] * conv_w[tap]
    xblk = xc.reshape(B, S, LRU_BLOCKS, LRU_BLOCK_W)
    r = jax.nn.sigmoid(jnp.einsum('bsnk,nkj->bsnj', xblk, w_r).reshape(B, S, D_RNN) + b_r)
    i = jax.nn.sigmoid(jnp.einsum('bsnk,nkj->bsnj', xblk, w_i).reshape(B, S, D_RNN) + b_i)
    log_a = -LRU_C * r.astype(jnp.float32) * jax.nn.softplus(-lam.astype(jnp.float32))
    a = jnp.exp(log_a)
    b = jnp.sqrt(-jnp.expm1(2.0 * log_a)) * (i * xc).astype(jnp.float32)
    _, hs = lax.associative_scan(_lin_rec_combine, (a, b), axis=1)
    y = jax.nn.gelu(gate) * hs.astype(h.dtype)
    return y @ w_out


def setup_inputs(seed: int = 0) -> dict:
    key = jax.random.key(seed)
    ks = jax.random.split(key, 24)
    D = D_MODEL
    f32 = jnp.float32
    nrm = lambda k, shape, s: jax.random.normal(k, shape, f32) * s
    x = nrm(ks[0], (BATCH, SEQ, D), 1.0)
    c = nrm(ks[1], (BATCH, D), 1.0)
    mod_w = nrm(ks[2], (DEPTH, D, N_SUB * 3 * D), 0.2 * D ** -0.5)
    mod_b = nrm(ks[3], (DEPTH, N_SUB * 3 * D), 0.05)
    norm_g = 1.0 + nrm(ks[4], (DEPTH, N_SUB, D), 0.02)
    ffn_w_gu = nrm(ks[5], (DEPTH, 2, D, 2 * D_FF), D ** -0.5)
    ffn_w_down = nrm(ks[6], (DEPTH, 2, D_FF, D), D_FF ** -0.5)
    sb_w_qkv = nrm(ks[7], (N_SB_LAYERS, D, 3 * D), D ** -0.5)
    sb_w_o = nrm(ks[8], (N_SB_LAYERS, D, D), D ** -0.5)
    lru_w_in = nrm(ks[9], (N_LRU_LAYERS, D, 2 * D_RNN), D ** -0.5)
    lru_conv_w = nrm(ks[10], (N_LRU_LAYERS, CONV_W, D_RNN), CONV_W ** -0.5)
    lru_conv_b = nrm(ks[11], (N_LRU_LAYERS, D_RNN), 0.02)
    lru_w_r = nrm(ks[12], (N_LRU_LAYERS, LRU_BLOCKS, LRU_BLOCK_W, LRU_BLOCK_W), LRU_BLOCK_W ** -0.5)
    lru_b_r = nrm(ks[13], (N_LRU_LAYERS, D_RNN), 0.1)
    lru_w_i = nrm(ks[14], (N_LRU_LAYERS, LRU_BLOCKS, LRU_BLOCK_W, LRU_BLOCK_W), LRU_BLOCK_W ** -0.5)
    lru_b_i = nrm(ks[15], (N_LRU_LAYERS, D_RNN), 0.1)
    a_c = jax.random.uniform(ks[16], (N_LRU_LAYERS, D_RNN), f32, 0.9, 0.999)
    a0 = a_c ** (1.0 / LRU_C)
    lru_lambda = jnp.log(a0) - jnp.log1p(-a0)
    lru_w_out = nrm(ks[17], (N_LRU_LAYERS, D_RNN, D), D_RNN ** -0.5)
    final_norm_g = 1.0 + nrm(ks[18], (D,), 0.02)
    return {"x": x, "c": c, "mod_w": mod_w, "mod_b": mod_b, "norm_g": norm_g,
            "ffn_w_gu": ffn_w_gu, "ffn_w_down": ffn_w_down,
            "sb_w_qkv": sb_w_qkv, "sb_w_o": sb_w_o,
            "lru_w_in": lru_w_in, "lru_conv_w": lru_conv_w, "lru_conv_b": lru_conv_b,
            "lru_w_r": lru_w_r, "lru_b_r": lru_b_r, "lru_w_i": lru_w_i, "lru_b_i": lru_b_i,
            "lru_lambda": lru_lambda, "lru_w_out": lru_w_out, "final_norm_g": final_norm_g}


def reference(x, c, mod_w, mod_b, norm_g, ffn_w_gu, ffn_w_down, sb_w_qkv, sb_w_o,
              lru_w_in, lru_conv_w, lru_conv_b, lru_w_r, lru_b_r, lru_w_i, lru_b_i,
              lru_lambda, lru_w_out, final_norm_g):
    B = x.shape[0]
    c_act = jax.nn.silu(c)
    for layer in range(DEPTH):
        mod = (c_act @ mod_w[layer] + mod_b[layer]).reshape(B, N_SUB, 3, D_MODEL)
        h, g = sublayer_input(x, norm_g[layer, 0], mod[:, 0])
        x = x + MACARON_W * g * swiglu(h, ffn_w_gu[layer, 0], ffn_w_down[layer, 0])
        h, g = sublayer_input(x, norm_g[layer, 1], mod[:, 1])
        j = layer // N_MIXERS
        if layer % N_MIXERS == 0:
            y = stick_breaking_attention(h, sb_w_qkv[j], sb_w_o[j])
        else:
            y = rglru_block(h, lru_w_in[j], lru_conv_w[j], lru_conv_b[j], lru_w_r[j], lru_b_r[j],
                            lru_w_i[j], lru_b_i[j], lru_lambda[j], lru_w_out[j])
        x = x + g * y
        h, g = sublayer_input(x, norm_g[layer, 2], mod[:, 2])
        x = x + MACARON_W * g * swiglu(h, ffn_w_gu[layer, 1], ffn_w_down[layer, 1])
    return rmsnorm(x, final_norm_g)
```

```python
import numpy as np
import concourse.bass as bass
import concourse.mybir as mybir
from concourse.bass_utils import run_bass_kernel_spmd

F32 = mybir.dt.float32
BF16 = mybir.dt.bfloat16
AF = mybir.ActivationFunctionType
ALU = mybir.AluOpType

EPS = 1e-6
LRU_C = 8.0
SERIALIZE = False


class Buf:
    __slots__ = ("name", "writer", "readers")

    def __init__(self, name):
        self.name = name
        self.writer = None
        self.readers = {}


class _Rec:
    __slots__ = ("kind", "fn", "inc", "sem", "val")

    def __init__(self, kind, fn=None, sem=None, val=None):
        self.kind = kind
        self.fn = fn
        self.inc = False
        self.sem = sem
        self.val = val


class Eng:
    def __init__(self, name, sem, raw_safe):
        self.name = name
        self.sem = sem
        self.recs = []
        self.count = 0
        self.last_op = None
        self.pending = []
        self.waited = {}
        self.raw_safe = raw_safe


class Ticket:
    __slots__ = ("eng", "rec", "val", "sem")

    def __init__(self, eng=None, rec=None, sem=None, val=None):
        self.eng = eng
        self.rec = rec
        self.sem = sem
        self.val = val


class DmaSem:
    def __init__(self, prog, name):
        self.sem = prog.nc.alloc_semaphore(name=name)
        self.count = 0


class Prog:
    def __init__(self, nc):
        self.nc = nc
        self.engs = {}
        self.n_wait = 0
        self.n_ops = 0

    def add_engine(self, name, raw_safe=False):
        sem = self.nc.alloc_semaphore(name="sem_" + name)
        e = Eng(name, sem, raw_safe)
        self.engs[name] = e
        return e

    def _resolve(self, t):
        if t.val is not None:
            return t.sem, t.val
        e = t.eng
        rec = t.rec
        if rec.val is None:
            last = e.last_op
            assert not last.inc, "lazy-inc bookkeeping"
            last.inc = True
            e.count += 1
            for r in e.pending:
                r.val = e.count
            e.pending = []
        t.sem, t.val = e.sem, rec.val
        assert t.val is not None
        return t.sem, t.val

    def _wait(self, eng, t):
        sem, val = self._resolve(t)
        key = id(sem)
        if eng.waited.get(key, 0) >= val:
            return
        eng.waited[key] = val
        eng.recs.append(_Rec("wait", sem=sem, val=val))
        self.n_wait += 1

    def op(self, eng, fn, reads=(), writes=(), dma_sem=None, ndma=1, eager=True):
        deps = []
        for b in reads:
            if b.writer is not None:
                deps.append(("raw", b.writer))
        for b in writes:
            for r in b.readers.values():
                deps.append(("war", r))
            if b.writer is not None:
                deps.append(("waw", b.writer))
        for kind, t in deps:
            if t.eng is eng and dma_sem is None:
                if eng.raw_safe:
                    continue
            self._wait(eng, t)
        rec = _Rec("op", fn=fn)
        eng.recs.append(rec)
        self.n_ops += 1
        if dma_sem is not None:
            dma_sem.count += 16 * ndma
            rec.sem = dma_sem.sem
            rec.kind = "dma"
            tk = Ticket(sem=dma_sem.sem, val=dma_sem.count)
            key = id(dma_sem)
        else:
            eng.pending.append(rec)
            eng.last_op = rec
            tk = Ticket(eng=eng, rec=rec)
            key = id(eng)
            if eager:
                rec.inc = True
                eng.count += 1
                for r in eng.pending:
                    r.val = eng.count
                eng.pending = []
        for b in reads:
            b.readers[key] = tk
        for b in writes:
            b.writer = tk
            b.readers = {}
        if SERIALIZE:
            for e2 in self.engs.values():
                if e2 is not eng or dma_sem is not None:
                    self._wait(e2, tk)
        return tk

    def wait_ticket(self, eng, t):
        self._wait(eng, t)

    def barrier(self, engs):
        tks = []
        for e in engs:
            if e.last_op is not None:
                tks.append(Ticket(eng=e, rec=e.last_op))
        for e in engs:
            for t in tks:
                if t.eng is not e:
                    self._wait(e, t)

    def emit(self):
        nc = self.nc
        with nc.Block() as block:
            for name, e in self.engs.items():
                def body(h, e=e):
                    for r in e.recs:
                        if r.kind == "wait":
                            h.wait_ge(r.sem, r.val)
                        elif r.kind == "dma":
                            for ins in r.fn(h):
                                ins.then_inc(r.sem, 16)
                        else:
                            ins = r.fn(h)
                            if r.inc:
                                ins.then_inc(e.sem, 1)
                getattr(block, name)(body)


class Cfg:
    def __init__(self, S=2048, D=1024, H=16, DFF=2816, DRNN=1280, DEPTH=2, GH=8):
        self.S, self.D, self.H, self.DFF, self.DRNN, self.DEPTH = S, D, H, DFF, DRNN, DEPTH
        self.KD = D // 128
        self.TC = 512
        self.NT = S // 512
        self.NB = S // 128
        self.KF = DFF // 128
        self.KR = DRNN // 128
        self.MODC = 9 * D
        self.MW = 512 if (9 * D) % 512 == 0 else 256
        self.GH = GH
        self.NSB = (DEPTH + 1) // 2
        self.NL = DEPTH // 2
        self.SLOT = 4096
        self.NSLOT = 3
        assert S % 512 == 0 and D % 256 == 0 and DFF % 256 == 0 and DRNN % 256 == 0
        assert H * 64 == D
        KD, KR = self.KD, self.KR
        o = 0
        self.o_c = o; o += KD
        self.o_modb = o; o += DEPTH * 9 * KD
        self.o_g = o; o += DEPTH * 3 * KD
        self.o_fg = o; o += KD
        self.o_lru = o
        self.lru_stride = KR * 4 + 4 * KR
        o += max(self.NL, 1) * self.lru_stride
        self.NSM = o
        self.NCONST = 5 * 128


def host_consts():
    j = np.arange(128)[:, None]
    s = np.arange(128)[None, :]
    ident = (j == s).astype(np.float32)
    ones = np.ones((128, 128), np.float32)
    negUI = -(j >= s).astype(np.float32)
    negLs = -(j < s).astype(np.float32)
    tri = (j < s).astype(np.float32)
    return np.ascontiguousarray(np.concatenate([ident, ones, negUI, negLs, tri], axis=1))


def colmajor(v):
    v = np.asarray(v, np.float32).reshape(-1, 128)
    return np.ascontiguousarray(v.T)


def host_smalls(cfg, b, inputs):
    sm = np.zeros((128, cfg.NSM), np.float32)
    KD, KR = cfg.KD, cfg.KR
    sm[:, cfg.o_c:cfg.o_c + KD] = colmajor(inputs["c"][b])
    for l in range(cfg.DEPTH):
        sm[:, cfg.o_modb + l * 9 * KD: cfg.o_modb + (l + 1) * 9 * KD] = colmajor(inputs["mod_b"][l])
        sm[:, cfg.o_g + l * 3 * KD: cfg.o_g + (l + 1) * 3 * KD] = colmajor(inputs["norm_g"][l].reshape(-1))
    sm[:, cfg.o_fg:cfg.o_fg + KD] = colmajor(inputs["final_norm_g"])
    for jl in range(cfg.NL):
        o = cfg.o_lru + jl * cfg.lru_stride
        cw = np.asarray(inputs["lru_conv_w"][jl], np.float32)
        sm[:, o:o + KR * 4] = cw.T.reshape(KR, 128, 4).transpose(1, 0, 2).reshape(128, KR * 4)
        o += KR * 4
        for nm in ("lru_conv_b", "lru_b_r", "lru_b_i", "lru_lambda"):
            sm[:, o:o + KR] = colmajor(inputs[nm][jl])
            o += KR
    return sm


def build(cfg, stages=None, final=True, plan=None):
    planning = plan is None
    S, D, KD, NT, NB, KF, KR, DFF, DRNN = cfg.S, cfg.D, cfg.KD, cfg.NT, cfg.NB, cfg.KF, cfg.KR, cfg.DFF, cfg.DRNN
    DEPTH = cfg.DEPTH
    if stages is None:
        stages = [(l, s) for l in range(DEPTH) for s in range(3)]
    nc = bass.Bass("TRN2", target_bir_lowering=False)
    T = {}

    def din(name, shape):
        T[name] = nc.dram_tensor(name, list(shape), F32, kind="ExternalInput").ap()

    din("x", [D, S])
    din("smalls", [128, cfg.NSM])
    din("consts", [128, cfg.NCONST])
    din("mod_w", [DEPTH, D, 9 * D])
    din("ffn_w_gu", [DEPTH, 2, D, 2 * DFF])
    din("ffn_w_down", [DEPTH, 2, DFF, D])
    din("sb_w_qkv", [cfg.NSB, D, 3 * D])
    din("sb_w_o", [cfg.NSB, D, D])
    din("lru_w_in", [max(cfg.NL, 1), D, 2 * DRNN])
    din("lru_w_r", [max(cfg.NL, 1), KR, 128, 128])
    din("lru_w_i", [max(cfg.NL, 1), KR, 128, 128])
    din("lru_w_out", [max(cfg.NL, 1), DRNN, D])
    out_ap = nc.dram_tensor("out", [D, S], F32, kind="ExternalOutput").ap()

    P = Prog(nc)
    pe = P.add_engine("tensor", raw_safe=True)
    act = P.add_engine("scalar")
    dve = P.add_engine("vector")
    pool = P.add_engine("gpsimd")
    sp = P.add_engine("sync")
    compute_engs = [pe, act, dve, pool]

    def OP(eng, fn, reads=(), writes=(), **kw):
        if planning:
            return None
        return P.op(eng, fn, reads, writes, **kw)

    def sb(name, shape, dt):
        return nc.alloc_sbuf_tensor(name, list(shape), dt)

    xT = sb("xT", [128, KD, S], F32)
    hT = sb("hT", [128, KD, S], BF16)
    xb = [Buf(f"x{t}") for t in range(NT)]
    hb = [Buf(f"h{t}") for t in range(NT)]
    sm = sb("sm", [128, cfg.NSM], F32)
    smb = Buf("sm")
    ident = sb("ident", [128, 128], F32)
    cbf = sb("cbf", [128, 3 * 128], BF16)
    tri = sb("tri", [128, 128], F32)
    cb = Buf("consts")
    ones_bf = cbf[:, 0:128]
    negUI = cbf[:, 128:256]
    negLs = cbf[:, 256:384]
    zeros_bf = sb("zeros", [128, 128], BF16)
    modv = [sb(f"modv{i}", [128, 9 * KD], F32) for i in range(2)]
    Asc = [sb(f"Asc{i}", [128, 3 * KD], F32) for i in range(2)]
    gsc = [sb(f"gsc{i}", [128, 3 * KD], F32) for i in range(2)]
    modb = [Buf("modv0"), Buf("modv1")]
    fgb = Buf("fgs")
    cact = sb("cact", [128, KD], BF16)
    cactb = Buf("cact")
    fgs = sb("fgs", [128, KD], F32)
    lruc = sb("lruc", [128, KR], F32)
    lrub = Buf("lruc")
    sqk = [sb(f"sqk{i}", [128, 512], BF16) for i in range(4)]
    sqkb = [Buf(f"sqk{i}") for i in range(4)]
    invs = [sb(f"inv{i}", [128, 512], F32) for i in range(2)]
    invsb = [Buf(f"inv{i}") for i in range(2)]
    ntmp = [sb(f"ntmp{i}", [128, 512], F32) for i in range(3)]
    ntmpb = [Buf(f"ntmp{i}") for i in range(3)]
    xstageb = [Buf(f"xstage{i}") for i in range(2)]
    c_sem = DmaSem(P, "csem")
    wslots = [sb(f"wslot{i}", [128, cfg.SLOT], BF16) for i in range(cfg.NSLOT)]
    wbufs = [Buf(f"wslot{i}") for i in range(cfg.NSLOT)]
    wsems = [DmaSem(P, f"wsem{i}") for i in range(cfg.NSLOT)]
    banks = [nc.alloc_psum_tensor(f"bank{i}", [128, 512], F32) for i in range(8)]
    bb = [Buf(f"bank{i}") for i in range(8)]

    units = [] if planning else plan
    wstate = {"consumed": 0, "issued": 0}

    def w_issue(i):
        slot = i % cfg.NSLOT
        tile = wslots[slot]
        pieces = units[i]

        def fn(h, tile=tile, pieces=pieces):
            res = []
            for (src, off, a, b) in pieces:
                dst = tile[:, off:off + a * b].rearrange("p (a b) -> p a b", a=a)
                res.append(h.dma_start(out=dst, in_=src(T)))
            return res
        P.op(pool, fn, writes=[wbufs[slot]], dma_sem=wsems[slot], ndma=len(pieces))

    def w_get(pieces):
        i = wstate["consumed"]
        wstate["consumed"] += 1
        if planning:
            units.append(pieces)
            return wslots[0], wbufs[0]
        while wstate["issued"] < min(len(units), i + cfg.NSLOT):
            w_issue(wstate["issued"])
            wstate["issued"] += 1
        slot = i % cfg.NSLOT
        return wslots[slot], wbufs[slot]

    def wview(tile, off, a, b):
        return tile[:, off:off + a * b].rearrange("p (a b) -> p a b", a=a)

    def MM(out, lhsT, rhs, start, stop, reads, writes, skip=False):
        if skip:
            OP(pe, lambda h: h.matmul(out, lhsT=lhsT, rhs=rhs, start=start, stop=stop, skip_group_check=True),
               reads, writes, eager=bool(stop))
        else:
            OP(pe, lambda h: h.matmul(out, lhsT=lhsT, rhs=rhs, start=start, stop=stop), reads, writes,
               eager=bool(stop))

    def ACT(out, in_, func, reads, writes, bias=None, scale=None):
        kw = {}
        if bias is not None:
            kw["bias"] = bias
        if scale is not None:
            kw["scale"] = scale
        return OP(act, lambda h: h.activation(out=out, in_=in_, func=func, **kw), reads, writes)

    def TT(eng, out, in0, in1, op, reads, writes):
        return OP(eng, lambda h: h.tensor_tensor(out=out, in0=in0, in1=in1, op=op), reads, writes)

    def TS(eng, out, in0, s1, s2, op0, op1, reads, writes):
        if op1 is None:
            return OP(eng, lambda h: h.tensor_scalar(out=out, in0=in0, scalar1=s1, scalar2=None, op0=op0),
                      reads, writes)
        return OP(eng, lambda h: h.tensor_scalar(out=out, in0=in0, scalar1=s1, scalar2=s2, op0=op0, op1=op1),
                  reads, writes)

    def STT(eng, out, in0, scalar, in1, op0, op1, reads, writes):
        return OP(eng, lambda h: h.scalar_tensor_tensor(out=out, in0=in0, scalar=scalar, in1=in1, op0=op0, op1=op1),
                  reads, writes)

    def CP(eng, out, in_, reads, writes):
        return OP(eng, lambda h: h.tensor_copy(out=out, in_=in_), reads, writes)

    def phase_barrier():
        if not planning:
            P.barrier(compute_engs)

    rot = {"gu": 0, "dn": 0, "nt": 0, "ev": 0, "nm": 0, "sq": 0, "nv": 0}

    def tl(name, shape, dt):
        rot["nm"] += 1
        return nc.sbuf_tensor(f"{name}_{rot['nm']}", list(shape), dt)

    def prologue():
        OP(sp, lambda h: [h.dma_start(out=sm[:], in_=T["smalls"][:, :]),
                          h.dma_start(out=ident[:], in_=T["consts"][:, 0:128]),
                          h.dma_start(out=tri[:], in_=T["consts"][:, 512:640])],
           writes=[smb, cb], dma_sem=c_sem, ndma=3)
        c2 = DmaSem(P, "csem2")
        OP(pool, lambda h: [h.dma_start(out=cbf[:], in_=T["consts"][:, 128:512])],
           writes=[cb], dma_sem=c2, ndma=1)
        OP(dve, lambda h: h.memset(zeros_bf[:], 0.0), writes=[cb])
        ACT(cact[:], sm[:, cfg.o_c:cfg.o_c + KD], AF.Silu, [smb], [cactb])
        x_load_chunks(range(0, 1))
        if stages:
            cur["l"] = stages[0][0]
            mod_require(stages[0][0], stages[0][1], 7)
            gate = wbufs[(wstate["consumed"] - 1) % cfg.NSLOT].writer if not planning else None
            if gate is not None:
                P.wait_ticket(sp, gate)
        x_load_chunks(range(1, NT))

    xload = {"done": NB, "last": None}
    xsems = [DmaSem(P, f"xs{i}") for i in range(NT)]

    def x_load_chunks(tcns):
        xsrc = T["x"].rearrange("(k p) s -> p k s", p=128)
        for tcn in tcns:
            cs = slice(tcn * 512, (tcn + 1) * 512)
            OP(sp, (lambda h, cs=cs: [h.dma_start(out=xT[:, :, cs], in_=xsrc[:, :, cs])]),
               writes=[xb[tcn]], dma_sem=xsems[tcn], ndma=1)

    mod_pending = []
    mod_done = set()
    mod_fin = set()
    MWc = cfg.MW
    nfl_ = MWc // 128
    NMU = cfg.MODC // MWc

    def mod_enqueue(l):
        for u in range(NMU):
            if (l, u) not in mod_done and (l, u) not in mod_pending:
                mod_pending.append((l, u))

    def mod_unit(l, u, bank_i):
        mod_done.add((l, u))
        pieces = [((lambda T, l=l, u=u: T["mod_w"][l].rearrange("(k p) c -> p k c", p=128)[:, :, u * MWc:(u + 1) * MWc]),
                   0, KD, MWc)]
        wt, wb = w_get(pieces)
        wv = wview(wt, 0, KD, MWc)
        mbank = banks[bank_i]
        for fl in range(nfl_):
            for k in range(KD):
                MM(mbank[:, fl:fl + 1], wv[:, k, fl * 128:(fl + 1) * 128], cact[:, k:k + 1],
                   k == 0, k == KD - 1, [wb, cactb], [bb[bank_i]])
        ob = cfg.o_modb + l * 9 * KD + u * nfl_
        TT(dve, modv[l % 2][:, u * nfl_:(u + 1) * nfl_], mbank[:, 0:nfl_], sm[:, ob:ob + nfl_], ALU.add,
           [bb[bank_i], smb], [modb[l % 2]])

    def mod_pump(n, bank_i):
        for _ in range(n):
            if not mod_pending:
                return
            l, u = mod_pending.pop(0)
            mod_unit(l, u, bank_i)

    def mod_require(l, sub, bank_i):
        u_lo = (sub * 3 * D) // MWc
        u_hi = ((sub + 1) * 3 * D + MWc - 1) // MWc
        for u in range(u_lo, u_hi):
            if (l, u) not in mod_done:
                if (l, u) in mod_pending:
                    mod_pending.remove((l, u))
                mod_unit(l, u, bank_i)
        if (l, sub) in mod_fin:
            return
        mod_fin.add((l, sub))
        mv, mb_ = modv[l % 2], modb[l % 2]
        og = cfg.o_g + l * 3 * KD
        sc = mv[:, (sub * 3 + 1) * KD:(sub * 3 + 2) * KD]
        gt = mv[:, (sub * 3 + 2) * KD:(sub * 3 + 3) * KD]
        A = Asc[l % 2][:, sub * KD:(sub + 1) * KD]
        STT(dve, A, sc, 1.0, sm[:, og + sub * KD: og + (sub + 1) * KD], ALU.add, ALU.mult, [mb_, smb], [mb_])
        TS(dve, A, A, float(np.sqrt(D)), None, ALU.mult, None, [mb_], [mb_])
        mw = 1.0 if sub == 1 else 0.5
        TS(dve, gsc[l % 2][:, sub * KD:(sub + 1) * KD], gt, 1.0, mw, ALU.add, ALU.mult, [mb_], [mb_])

    def norm_stats(tcn):
        cs = slice(tcn * 512, (tcn + 1) * 512)
        for k in range(KD):
            r = rot["sq"] % 4
            rot["sq"] += 1
            ACT(sqk[r][:], xT[:, k, cs], AF.Square, [xb[tcn]], [sqkb[r]])
            MM(banks[7][:], ones_bf, sqk[r][:], k == 0, k == KD - 1, [sqkb[r], cb], [bb[7]])
        iv = rot["nv"] % 2
        rot["nv"] += 1
        ACT(invs[iv][:], banks[7][:], AF.Ln, [bb[7]], [invsb[iv]], bias=float(D * EPS), scale=1.0)
        ACT(invs[iv][:], invs[iv][:], AF.Exp, [invsb[iv]], [invsb[iv]], scale=-0.5)
        return iv

    def norm_apply(tcn, iv, Acols, shiftcols, dst_fn, dstbufs, areads):
        cs = slice(tcn * 512, (tcn + 1) * 512)
        if tcn == NT - 1 and xload["last"] is not None:
            if not planning:
                P.wait_ticket(act, xload["last"])
                P.wait_ticket(dve, xload["last"])
            xload["last"] = None
        for k in range(KD):
            if shiftcols is None:
                STT(dve, dst_fn(k, tcn), xT[:, k, cs], Acols[:, k:k + 1], invs[iv][:], ALU.mult, ALU.mult,
                    [xb[tcn], invsb[iv]] + areads, [dstbufs[tcn]])
                continue
            i = rot["nt"] % 3
            rot["nt"] += 1
            STT(dve, ntmp[i][:], xT[:, k, cs], Acols[:, k:k + 1], invs[iv][:], ALU.mult, ALU.mult,
                [xb[tcn], invsb[iv]] + areads, [ntmpb[i]])
            if k % 2 == 0:
                ACT(dst_fn(k, tcn), ntmp[i][:], AF.Identity, [ntmpb[i]] + areads, [dstbufs[tcn]],
                    bias=shiftcols[:, k:k + 1], scale=1.0)
            else:
                TS(dve, dst_fn(k, tcn), ntmp[i][:], shiftcols[:, k:k + 1], None, ALU.add, None,
                   [ntmpb[i]] + areads, [dstbufs[tcn]])

    def norm_phase(Acols, shiftcols, dst_fn, dstbufs, areads, after_apply=None):
        iv = norm_stats(0)
        for tcn in range(NT):
            iv_next = norm_stats(tcn + 1) if tcn + 1 < NT else None
            norm_apply(tcn, iv, Acols, shiftcols, dst_fn, dstbufs, areads)
            if after_apply is not None:
                after_apply(tcn)
            iv = iv_next

    cur = {"l": 0}

    def sub_norm(sub, after_apply=None):
        l = cur["l"]
        mod_require(l, sub, 7)
        norm_phase(Asc[l % 2][:, sub * KD:(sub + 1) * KD], modv[l % 2][:, (sub * 3) * KD:(sub * 3 + 1) * KD],
                   lambda k, tcn: hT[:, k, tcn * 512:(tcn + 1) * 512], hb, [modb[l % 2]], after_apply=after_apply)

    def resid_update(bank_i, o, tcn, sub):
        l = cur["l"]
        cs = slice(tcn * 512, (tcn + 1) * 512)
        STT(dve, xT[:, o, cs], banks[bank_i][:], gsc[l % 2][:, sub * KD + o: sub * KD + o + 1], xT[:, o, cs],
            ALU.mult, ALU.add, [bb[bank_i], xb[tcn], modb[l % 2]], [xb[tcn]])

    def ffn_phase(l, f, sub):
        mod_require(l, sub, 7)
        GH = cfg.GH
        groups = []
        g0 = 0
        while g0 < KF:
            gn = min(GH, KF - g0)
            groups.append((g0, gn))
            g0 += gn
        with tl("actT", [128, GH, S], BF16) as actT, \
                tl("sg0", [128, 512], F32) as sg0, tl("sg1", [128, 512], F32) as sg1:
            sgs = [sg0, sg1]
            sgb = [Buf("sg0"), Buf("sg1")]
            actb = [Buf(f"act{t}") for t in range(NT)]

            def gu_get(g0, pu):
                c0 = (g0 + 2 * pu) * 128
                pieces = [
                    ((lambda T, l=l, f=f, c0=c0: T["ffn_w_gu"][l, f].rearrange("(k p) c -> p k c", p=128)[:, :, c0:c0 + 256]),
                     0, KD, 256),
                    ((lambda T, l=l, f=f, c0=c0: T["ffn_w_gu"][l, f].rearrange("(k p) c -> p k c", p=128)[:, :, DFF + c0:DFF + c0 + 256]),
                     KD * 256, KD, 256)]
                wt, wb = w_get(pieces)
                return wview(wt, 0, KD, 256), wview(wt, KD * 256, KD, 256), wb

            def gu_emit(wg, wu, wb, pu, cc, tcn):
                fa = 2 * pu + cc
                cs = slice(tcn * 512, (tcn + 1) * 512)
                r = rot["gu"] % 2
                rot["gu"] += 1
                gi, ui = r, 2 + r
                for k in range(KD):
                    MM(banks[gi][:], wg[:, k, cc * 128:(cc + 1) * 128], hT[:, k, cs], k == 0, k == KD - 1,
                       [wb, hb[tcn]], [bb[gi]])
                for k in range(KD):
                    MM(banks[ui][:], wu[:, k, cc * 128:(cc + 1) * 128], hT[:, k, cs], k == 0, k == KD - 1,
                       [wb, hb[tcn]], [bb[ui]])
                ACT(sgs[r][:], banks[gi][:], AF.Silu, [bb[gi]], [sgb[r]])
                TT(dve, actT[:, fa, cs], banks[ui][:], sgs[r][:], ALU.mult, [bb[ui], sgb[r]], [actb[tcn]])

            wg0, wu0, wb0 = gu_get(groups[0][0], 0)

            def first_unit(tcn):
                for cc in range(2):
                    gu_emit(wg0, wu0, wb0, 0, cc, tcn)
            sub_norm(sub, after_apply=first_unit)
            mod_pump(1, 7)
            for gidx, (g0, gn) in enumerate(groups):
                assert gn % 2 == 0
                for pu in range(gn // 2):
                    if gidx == 0 and pu == 0:
                        continue
                    wg, wu, wb = gu_get(g0, pu)
                    for cc in range(2):
                        for tcn in range(NT):
                            gu_emit(wg, wu, wb, pu, cc, tcn)
                    mod_pump(1, 7)
                for du in range(D // 256):
                    pieces = [((lambda T, l=l, f=f, g0=g0, gn=gn, du=du:
                                T["ffn_w_down"][l, f][g0 * 128:(g0 + gn) * 128, :].rearrange("(k p) c -> p k c", p=128)[:, :, du * 256:(du + 1) * 256]),
                               0, gn, 256)]
                    wt, wb = w_get(pieces)
                    wd = wview(wt, 0, gn, 256)
                    for oc in range(2):
                        o = du * 2 + oc
                        for tcn in range(NT):
                            cs = slice(tcn * 512, (tcn + 1) * 512)
                            bi = 4 + (rot["dn"] % 2)
                            rot["dn"] += 1
                            for k in range(gn):
                                MM(banks[bi][:], wd[:, k, oc * 128:(oc + 1) * 128], actT[:, k, cs], k == 0, k == gn - 1,
                                   [wb, actb[tcn]], [bb[bi]])
                            resid_update(bi, o, tcn, sub)
                    mod_pump(1, 7)
            phase_barrier()

    def attn_phase(l, j):
        sub = 1
        mod_require(cur["l"], sub, 7)
        with tl("qT", [128, 2, S], BF16) as qT, tl("kT", [128, 2, S], BF16) as kT, \
                tl("vv", [128, NB, 256], BF16) as vv, tl("oT", [128, 2, S], BF16) as oT, \
                tl("e_sb", [128, 4, 512], F32) as e_sb, tl("sp_sb", [128, 4, 512], BF16) as sp_sb, \
                tl("E_sb", [128, 4, 512], F32) as E_sb, tl("w_sb", [128, 4, 512], BF16) as w_sb:
            qb_, kb_, vb_, ob_ = Buf("qT"), Buf("kT"), Buf("vv"), Buf("oT")
            eb = [Buf(f"e{i}") for i in range(4)]
            spb = [Buf(f"sp{i}") for i in range(4)]
            Eb = [Buf(f"E{i}") for i in range(4)]
            wb_ = [Buf(f"w{i}") for i in range(4)]
            pj = {"i": 0}

            def pbank():
                pj["i"] += 1
                return pj["i"] % 2
            def qk_get(hg):
                c0 = hg * 256
                pieces = [
                    ((lambda T, j=j, c0=c0: T["sb_w_qkv"][j].rearrange("(k p) c -> p k c", p=128)[:, :, c0:c0 + 256]), 0, KD, 256),
                    ((lambda T, j=j, c0=c0: T["sb_w_qkv"][j].rearrange("(k p) c -> p k c", p=128)[:, :, D + c0:D + c0 + 256]), KD * 256, KD, 256)]
                wt, wb = w_get(pieces)
                return wview(wt, 0, KD, 256), wview(wt, KD * 256, KD, 256), wb

            def qk_emit(wq, wk, wb, hp2, tcn):
                cs = slice(tcn * 512, (tcn + 1) * 512)
                bi = pbank()
                for k in range(KD):
                    MM(banks[bi][:], wq[:, k, hp2 * 128:(hp2 + 1) * 128], hT[:, k, cs], k == 0, k == KD - 1,
                       [wb, hb[tcn]], [bb[bi]])
                ACT(qT[:, hp2, cs], banks[bi][:], AF.Identity, [bb[bi]], [qb_], bias=0.0, scale=0.125)
                bi = pbank()
                for k in range(KD):
                    MM(banks[bi][:], wk[:, k, hp2 * 128:(hp2 + 1) * 128], hT[:, k, cs], k == 0, k == KD - 1,
                       [wb, hb[tcn]], [bb[bi]])
                CP(dve, kT[:, hp2, cs], banks[bi][:], [bb[bi]], [kb_])

            wq0, wk0, wb0 = qk_get(0)

            def first_qk(tcn):
                for hp2 in range(2):
                    qk_emit(wq0, wk0, wb0, hp2, tcn)
            sub_norm(sub, after_apply=first_qk)
            for hg in range(D // 256):
                c0 = hg * 256
                if hg > 0:
                    wq, wk, wb = qk_get(hg)
                    for hp2 in range(2):
                        for tcn in range(NT):
                            qk_emit(wq, wk, wb, hp2, tcn)
                pieces = [((lambda T, j=j, c0=c0: T["sb_w_qkv"][j].rearrange("(k p) c -> p k c", p=128)[:, :, 2 * D + c0:2 * D + c0 + 256]), 0, KD, 256)]
                wt, wb = w_get(pieces)
                wvv = wview(wt, 0, KD, 256)
                for tb in range(NB):
                    bi = pbank()
                    for k in range(KD):
                        MM(banks[bi][:, 0:256], hT[:, k, tb * 128:(tb + 1) * 128], wvv[:, k, :], k == 0, k == KD - 1,
                           [wb, hb[tb // 4]], [bb[bi]])
                    if tb % 2:
                        CP(dve, vv[:, tb, :], banks[bi][:, 0:256], [bb[bi]], [vb_])
                    else:
                        ACT(vv[:, tb, :], banks[bi][:, 0:256], AF.Copy, [bb[bi]], [vb_])
                pieces = [((lambda T, j=j, c0=c0: T["sb_w_o"][j][c0:c0 + 256, :].rearrange("(k p) c -> p k c", p=128)), 0, 2, D)]
                wot, wob = w_get(pieces)
                wo = wview(wot, 0, 2, D)
                prs = [slice(0, 64), slice(64, 128)]
                for qc in range(NT):
                    q0 = qc * 512
                    for bi in (2, 3, 4, 5, 6, 7):
                        MM(banks[bi][:], zeros_bf[:, :], hT[:, 0, 0:512], True, True, [cb, hb[0]], [bb[bi]])
                    kbs = list(range(4 * qc + 3, -1, -1))

                    def geom(kb, qc=qc):
                        jd = kb - 4 * qc
                        cst = 128 * jd if jd > 0 else 0
                        return jd, cst

                    def zmm(s_, kb, q0=q0):
                        jd, cst = geom(kb)
                        hp2, pr = s_ // 2, prs[s_ % 2]
                        zb = s_ % 2
                        MM(banks[zb][:, cst:512], kT[pr, hp2, kb * 128:(kb + 1) * 128], qT[pr, hp2, q0 + cst:q0 + 512],
                           True, True, [kb_, qb_], [bb[zb]])

                    def e_op(s_, kb):
                        jd, cst = geom(kb)
                        zb = s_ % 2
                        ACT(e_sb[:, s_, cst:512], banks[zb][:, cst:512], AF.Exp, [bb[zb]], [eb[s_]])

                    def sp_op(s_, kb):
                        jd, cst = geom(kb)
                        ACT(sp_sb[:, s_, cst:512], e_sb[:, s_, cst:512], AF.Ln, [eb[s_]], [spb[s_]], bias=1.0, scale=1.0)
                        if jd >= 0:
                            TT(dve, sp_sb[:, s_, cst:cst + 128], sp_sb[:, s_, cst:cst + 128], tri[:, :], ALU.mult,
                               [spb[s_], cb], [spb[s_]])

                    def r1_op(s_, kb):
                        jd, cst = geom(kb)
                        MM(banks[2 + s_][:, cst:512], negUI, sp_sb[:, s_, cst:512], False, True, [cb, spb[s_]], [bb[2 + s_]], skip=True)

                    def E_op(s_, kb):
                        jd, cst = geom(kb)
                        ACT(E_sb[:, s_, cst:512], banks[2 + s_][:, cst:512], AF.Exp, [bb[2 + s_]], [Eb[s_]])

                    def w_op(s_, kb):
                        jd, cst = geom(kb)
                        TT(dve, w_sb[:, s_, cst:512], e_sb[:, s_, cst:512], E_sb[:, s_, cst:512], ALU.mult,
                           [eb[s_], Eb[s_]], [wb_[s_]])
                        if jd >= 0:
                            TT(dve, w_sb[:, s_, cst:cst + 128], w_sb[:, s_, cst:cst + 128], tri[:, :], ALU.mult,
                               [wb_[s_], cb], [wb_[s_]])

                    def o_op(s_, kb):
                        jd, cst = geom(kb)
                        hp2, hd = s_ // 2, s_ % 2
                        MM(banks[6 + hp2][hd * 64:(hd + 1) * 64, cst:512], vv[:, kb, hp2 * 128 + hd * 64: hp2 * 128 + (hd + 1) * 64],
                           w_sb[:, s_, cst:512], False, True, [vb_, wb_[s_]], [bb[6 + hp2]], skip=True)

                    def r2_op(s_, kb):
                        jd, cst = geom(kb)
                        MM(banks[2 + s_][:, cst:512], negLs, sp_sb[:, s_, cst:512], False, True, [cb, spb[s_]], [bb[2 + s_]], skip=True)

                    zmm(0, kbs[0])
                    zmm(1, kbs[0])
                    for ii, kb in enumerate(kbs):
                        last = ii + 1 == len(kbs)
                        e_op(0, kb)
                        e_op(1, kb)
                        zmm(2, kb)
                        zmm(3, kb)
                        sp_op(0, kb)
                        sp_op(1, kb)
                        r1_op(0, kb)
                        r1_op(1, kb)
                        e_op(2, kb)
                        e_op(3, kb)
                        if not last:
                            zmm(0, kbs[ii + 1])
                            zmm(1, kbs[ii + 1])
                        sp_op(2, kb)
                        sp_op(3, kb)
                        r1_op(2, kb)
                        r1_op(3, kb)
                        for s_ in range(4):
                            E_op(s_, kb)
                        for s_ in range(4):
                            w_op(s_, kb)
                        for pair in (0, 2):
                            o_op(pair, kb)
                            o_op(pair + 1, kb)
                            if not last:
                                r2_op(pair, kb)
                                r2_op(pair + 1, kb)
                    ACT(oT[:, 0, q0:q0 + 512], banks[6][:, :], AF.Copy, [bb[6]], [ob_])
                    CP(dve, oT[:, 1, q0:q0 + 512], banks[7][:, :], [bb[7]], [ob_])
                for o in range(KD):
                    for tcn in range(NT):
                        cs = slice(tcn * 512, (tcn + 1) * 512)
                        bi = pbank()
                        for k in range(2):
                            MM(banks[bi][:], wo[:, k, o * 128:(o + 1) * 128], oT[:, k, cs], k == 0, k == 1, [wob, ob_], [bb[bi]])
                        resid_update(bi, o, tcn, sub)
                mod_pump(5, 0)
            phase_barrier()

    def lru_phase(l, j):
        sub = 1
        sub_norm(sub)
        ol = cfg.o_lru + j * cfg.lru_stride
        o_cw = ol
        o_cb = ol + KR * 4
        o_br = o_cb + KR
        o_bi = o_br + KR
        o_lam = o_bi + KR
        NH = 2
        TH = S // NH
        NTH = TH // 512
        with tl("xbp0", [128, TH + 4], F32) as xbp0, tl("xbp1", [128, TH + 4], F32) as xbp1, \
                tl("xc0", [128, TH], F32) as xc0, tl("xc1", [128, TH], F32) as xc1, \
                tl("xcb0", [128, TH], BF16) as xcb0, tl("xcb1", [128, TH], BF16) as xcb1, \
                tl("ra0", [128, TH], F32) as ra0, tl("ra1", [128, TH], F32) as ra1, \
                tl("ib0", [128, TH], F32) as ib0, tl("ib1", [128, TH], F32) as ib1, \
                tl("yT", [128, 4, TH], BF16) as yT, tl("wri", [128, 2 * KR * 128], BF16) as wri, \
                tl("lt", [128, KR], F32) as lt, tl("halo", [128, KR, 4], F32) as halo, \
                tl("hlast", [128, KR], F32) as hlast, \
                tl("hsB0", [128, TH], F32) as hsB0, tl("hsB1", [128, TH], F32) as hsB1, \
                tl("ggB0", [128, TH], F32) as ggB0, tl("ggB1", [128, TH], F32) as ggB1:
            hsB = [hsB0, hsB1]
            ggB = [ggB0, ggB1]
            Bh = [Buf("hsB0"), Buf("hsB1")]
            Bg = [Buf("ggB0"), Buf("ggB1")]
            xbp = [xbp0, xbp1]
            xc = [xc0, xc1]
            xcb = [xcb0, xcb1]
            ra = [ra0, ra1]
            ib = [ib0, ib1]
            t2 = [xbp0[:, 4:4 + TH], xbp1[:, 4:4 + TH]]
            B_ = {n: Buf(n) for n in ["yT", "wri", "lt", "halo", "hlast"]}
            Bx = [Buf("xbp0"), Buf("xbp1")]
            Bc = [Buf("xc0"), Buf("xc1")]
            Bcb = [Buf("xcb0"), Buf("xcb1")]
            Br = [Buf("ra0"), Buf("ra1")]
            Bi = [Buf("ib0"), Buf("ib1")]
            cengs = [dve, dve]
            ACT(lt[:], sm[:, o_lam:o_lam + KR], AF.Exp, [smb], [B_["lt"]], scale=-1.0)
            ACT(lt[:], lt[:], AF.Ln, [B_["lt"]], [B_["lt"]], bias=1.0, scale=1.0)
            TS(dve, lruc[:], lt[:], -LRU_C, None, ALU.mult, None, [B_["lt"]], [lrub])
            pieces = [((lambda T, j=j: T["lru_w_r"][j].rearrange("n k c -> k n c")), 0, KR, 128),
                      ((lambda T, j=j: T["lru_w_i"][j].rearrange("n k c -> k n c")), KR * 128, KR, 128)]
            wt, wb = w_get(pieces)
            CP(dve, wri[:, :], wt[:, 0:2 * KR * 128], [wb], [B_["wri"]])
            wr = wri[:, 0:KR * 128].rearrange("p (a b) -> p a b", a=KR)
            wi = wri[:, KR * 128:2 * KR * 128].rearrange("p (a b) -> p a b", a=KR)
            OP(dve, lambda h: h.memset(halo[:], 0.0), writes=[B_["halo"]])
            OP(dve, lambda h: h.memset(hlast[:], 0.0), writes=[B_["hlast"]])
            its = [(half, ug) for half in range(NH) for ug in range(KR // 2)]

            def A1(it):
                half, ug = it
                t0 = half * TH
                c0 = ug * 256
                ns = [ug * 2, ug * 2 + 1]
                pieces = [((lambda T, j=j, c0=c0: T["lru_w_in"][j].rearrange("(k p) c -> p k c", p=128)[:, :, DRNN + c0:DRNN + c0 + 256]), 0, KD, 256)]
                wt, wb = w_get(pieces)
                wxb = wview(wt, 0, KD, 256)
                for c in range(2):
                    CP(dve, xbp[c][:, 0:4], halo[:, ns[c], :], [B_["halo"]], [Bx[c]])
                for c in range(2):
                    for tch in range(NTH):
                        cs = slice(t0 + tch * 512, t0 + (tch + 1) * 512)
                        bi = 6 + (rot["ev"] % 2)
                        rot["ev"] += 1
                        for k in range(KD):
                            MM(banks[bi][:], wxb[:, k, c * 128:(c + 1) * 128], hT[:, k, cs], k == 0, k == KD - 1,
                               [wb, hb[cs.start // 512]], [bb[bi]])
                        ACT(xbp[c][:, 4 + tch * 512: 4 + (tch + 1) * 512], banks[bi][:], AF.Copy, [bb[bi]], [Bx[c]])

            def A2a(it):
                half, ug = it
                ns = [ug * 2, ug * 2 + 1]
                for c in range(2):
                    n = ns[c]
                    TS(dve, xc[c][:, :], xbp[c][:, 1:1 + TH], sm[:, o_cw + n * 4: o_cw + n * 4 + 1], sm[:, o_cb + n: o_cb + n + 1],
                       ALU.mult, ALU.add, [Bx[c], smb], [Bc[c]])
                for tap in range(1, 4):
                    for c in range(2):
                        n = ns[c]
                        STT(dve, xc[c][:, :], xbp[c][:, 1 + tap:1 + tap + TH], sm[:, o_cw + n * 4 + tap: o_cw + n * 4 + tap + 1], xc[c][:, :],
                            ALU.mult, ALU.add, [Bx[c], Bc[c], smb], [Bc[c]])
                for c in range(2):
                    ACT(halo[:, ns[c], :], xbp[c][:, TH:TH + 4], AF.Copy, [Bx[c]], [B_["halo"]])
                    ACT(xcb[c][:, :], xc[c][:, :], AF.Copy, [Bc[c]], [Bcb[c]])

            def A2b(it):
                half, ug = it
                ns = [ug * 2, ug * 2 + 1]
                for c in range(2):
                    n = ns[c]
                    for tch in range(NTH):
                        ls = slice(tch * 512, (tch + 1) * 512)
                        bi = 4 + (rot["dn"] % 2)
                        rot["dn"] += 1
                        MM(banks[bi][:], wr[:, n, :], xcb[c][:, ls], True, True, [B_["wri"], Bcb[c]], [bb[bi]])
                        ACT(ra[c][:, ls], banks[bi][:], AF.Sigmoid, [bb[bi], smb], [Br[c]], bias=sm[:, o_br + n:o_br + n + 1], scale=1.0)
                        bi = 4 + (rot["dn"] % 2)
                        rot["dn"] += 1
                        MM(banks[bi][:], wi[:, n, :], xcb[c][:, ls], True, True, [B_["wri"], Bcb[c]], [bb[bi]])
                        ACT(ib[c][:, ls], banks[bi][:], AF.Sigmoid, [bb[bi], smb], [Bi[c]], bias=sm[:, o_bi + n:o_bi + n + 1], scale=1.0)
                for c in range(2):
                    ACT(ra[c][:, :], ra[c][:, :], AF.Exp, [Br[c], lrub], [Br[c]], scale=lruc[:, ns[c]:ns[c] + 1])
                for c in range(2):
                    TT(pool, ib[c][:, :], ib[c][:, :], xc[c][:, :], ALU.mult, [Bi[c], Bc[c]], [Bi[c]])
                    TT(pool, xc[c][:, :], ra[c][:, :], ra[c][:, :], ALU.mult, [Br[c]], [Bc[c]])
                for c in range(2):
                    ACT(xc[c][:, :], xc[c][:, :], AF.Sqrt, [Bc[c]], [Bc[c]], bias=1.0, scale=-1.0)

            def A2c(it):
                for c in range(2):
                    TT(dve, ib[c][:, :], ib[c][:, :], xc[c][:, :], ALU.mult, [Bi[c], Bc[c]], [Bi[c]])

            def A3(it):
                half, ug = it
                ns = [ug * 2, ug * 2 + 1]
                for c in range(2):
                    n = ns[c]
                    OP(dve, (lambda h, c=c, n=n: h.tensor_tensor_scan(out=hsB[c][:, :], data0=ra[c][:, :], data1=ib[c][:, :],
                                                                   initial=hlast[:, n:n + 1], op0=ALU.mult, op1=ALU.add)),
                       reads=[Br[c], Bi[c], B_["hlast"]], writes=[Bh[c]])
                    ACT(hlast[:, n:n + 1], hsB[c][:, TH - 1:TH], AF.Copy, [Bh[c]], [B_["hlast"]])

            def gitems():
                return [(tch, c) for tch in range(NTH) for c in range(2)]

            def B1a(it):
                half, ug = it
                t0 = half * TH
                c0 = ug * 256
                pieces = [((lambda T, j=j, c0=c0: T["lru_w_in"][j].rearrange("(k p) c -> p k c", p=128)[:, :, c0:c0 + 256]), 0, KD, 256)]
                wt, wb = w_get(pieces)
                wga = wview(wt, 0, KD, 256)
                for b_, (tch, c) in enumerate(gitems()):
                    cs = slice(t0 + tch * 512, t0 + (tch + 1) * 512)
                    for k in range(KD):
                        MM(banks[b_][:], wga[:, k, c * 128:(c + 1) * 128], hT[:, k, cs], k == 0, k == KD - 1,
                           [wb, hb[cs.start // 512]], [bb[b_]])
                for b_, (tch, c) in enumerate(gitems()):
                    ls = slice(tch * 512, (tch + 1) * 512)
                    ACT(ggB[c][:, ls], banks[b_][:], AF.Square, [bb[b_]], [Bg[c]], scale=0.21145921366322695)

            def B1b(it):
                for b_, (tch, c) in enumerate(gitems()):
                    ls = slice(tch * 512, (tch + 1) * 512)
                    STT(dve, ggB[c][:, ls], ggB[c][:, ls], 1.0, banks[b_][:], ALU.add, ALU.mult, [Bg[c], bb[b_]], [Bg[c]])
                for b_, (tch, c) in enumerate(gitems()):
                    ls = slice(tch * 512, (tch + 1) * 512)
                    ACT(ggB[c][:, ls], ggB[c][:, ls], AF.Sigmoid, [Bg[c]], [Bg[c]], scale=1.5957691216057308)
                for b_, (tch, c) in enumerate(gitems()):
                    ls = slice(tch * 512, (tch + 1) * 512)
                    TT(dve, ggB[c][:, ls], ggB[c][:, ls], banks[b_][:], ALU.mult, [Bg[c], bb[b_]], [Bg[c]])

            NUG = KR // 2

            def B2(it):
                half, ug = it
                yo = 2 * (ug % 2)
                for tch in range(NTH):
                    for c in range(2):
                        ls = slice(tch * 512, (tch + 1) * 512)
                        TT(dve, yT[:, yo + c, ls], ggB[c][:, ls], hsB[c][:, ls], ALU.mult, [Bg[c], Bh[c]], [B_["yT"]])

            def B3(it):
                half, ug = it
                if ug % 2 == 0 and ug + 1 < NUG:
                    return
                u0 = ug - 1 if ug % 2 == 1 else ug
                nk = 4 if ug % 2 == 1 else 2
                ko = 0 if ug % 2 == 1 else 2 * (ug % 2)
                c0 = u0 * 256
                pieces = [((lambda T, j=j, c0=c0, nk=nk: T["lru_w_out"][j][c0:c0 + nk * 128, :].rearrange("(k p) c -> p k c", p=128)), 0, nk, D)]
                wot, wob = w_get(pieces)
                wo = wview(wot, 0, nk, D)
                for o in range(KD):
                    for tch in range(NTH):
                        ls = slice(tch * 512, (tch + 1) * 512)
                        bi = 6 + (rot["ev"] % 2)
                        rot["ev"] += 1
                        for k in range(nk):
                            MM(banks[bi][:], wo[:, k, o * 128:(o + 1) * 128], yT[:, ko + k, ls], k == 0, k == nk - 1,
                               [wob, B_["yT"]], [bb[bi]])
                        resid_update(bi, o, half * NTH + tch, sub)

            n_it = len(its)
            A1(its[0])
            A2a(its[0])
            A2b(its[0])
            A2c(its[0])
            A3(its[0])
            if n_it > 1:
                A1(its[1])
            for i in range(n_it):
                B1a(its[i])
                if i + 1 < n_it:
                    A2a(its[i + 1])
                B1b(its[i])
                if i + 1 < n_it:
                    A2b(its[i + 1])
                B2(its[i])
                B3(its[i])
                if i + 2 < n_it:
                    A1(its[i + 2])
                if i + 1 < n_it:
                    A2c(its[i + 1])
                    A3(its[i + 1])
            phase_barrier()

    def gelu_tanh(gg, ggb, cs, bank, bankb):
        ACT(gg[:, cs], bank[:], AF.Square, [bankb], [ggb])
        TS(dve, gg[:, cs], gg[:, cs], 0.044715, 1.0, ALU.mult, ALU.add, [ggb], [ggb])
        TT(dve, gg[:, cs], gg[:, cs], bank[:], ALU.mult, [ggb, bankb], [ggb])
        ACT(gg[:, cs], gg[:, cs], AF.Sigmoid, [ggb], [ggb], scale=1.5957691216057308)
        TT(dve, gg[:, cs], gg[:, cs], bank[:], ALU.mult, [ggb, bankb], [ggb])

    def epilogue():
        odst = out_ap.rearrange("(k p) s -> p k s", p=128)
        with tl("onT0", [128, KD, 512], F32) as onT0, tl("onT1", [128, KD, 512], F32) as onT1:
            onTs = [onT0, onT1]
            onb = [Buf("onT0"), Buf("onT1")]
            osem = [DmaSem(P, "osem0"), DmaSem(P, "osem1")]
            last = {}
            if final:
                TS(dve, fgs[:, :], sm[:, cfg.o_fg:cfg.o_fg + KD], float(np.sqrt(D)), None, ALU.mult, None, [smb], [fgb])
                iv = norm_stats(0)
            for tcn in range(NT):
                cs = slice(tcn * 512, (tcn + 1) * 512)
                if final:
                    iv_next = norm_stats(tcn + 1) if tcn + 1 < NT else None
                    o_ = onTs[tcn % 2]
                    norm_apply(tcn, iv, fgs, None, lambda k, t_, o_=o_: o_[:, k, :], [onb[tcn % 2]] * NT, [fgb])
                    iv = iv_next
                    tk = OP(sp, (lambda h, o_=o_, cs=cs: [h.dma_start(out=odst[:, :, cs], in_=o_[:, :, :])]),
                            reads=[onb[tcn % 2]], dma_sem=osem[tcn % 2], ndma=1)
                else:
                    tk = OP(sp, (lambda h, cs=cs: [h.dma_start(out=odst[:, :, cs], in_=xT[:, :, cs])]),
                            reads=[xb[tcn]], dma_sem=osem[tcn % 2], ndma=1)
                last[tcn % 2] = tk
            if not planning:
                for tk in last.values():
                    P.wait_ticket(sp, tk)
                for tk in last.values():
                    P.wait_ticket(act, tk)

    prologue()
    layers_in = []
    for (l, s_) in stages:
        if l not in layers_in:
            layers_in.append(l)
    if layers_in:
        mod_enqueue(layers_in[0])
    for (l, s_) in stages:
        cur["l"] = l
        nxt = [m for m in layers_in if m > l]
        cur["next"] = nxt[0] if nxt else None
        if s_ == 0:
            ffn_phase(l, 0, 0)
        elif s_ == 2:
            ffn_phase(l, 1, 2)
        else:
            if cur["next"] is not None:
                mod_enqueue(cur["next"])
            if l % 2 == 0:
                attn_phase(l, l // 2)
            else:
                lru_phase(l, l // 2)
    epilogue()
    if planning:
        return units
    assert wstate["consumed"] == len(units), (wstate, len(units))
    P.emit()
    return nc, P


def build_program(cfg, stages=None, final=True):
    plan = build(cfg, stages, final, plan=None)
    return build(cfg, stages, final, plan=plan)


_W_NAMES = ["mod_w", "ffn_w_gu", "ffn_w_down", "sb_w_qkv", "sb_w_o", "lru_w_in", "lru_w_r", "lru_w_i", "lru_w_out"]


def make_in_maps(cfg, inputs, B):
    consts = host_consts()
    shared = {n: np.ascontiguousarray(np.asarray(inputs[n], np.float32)) for n in _W_NAMES}
    in_maps = []
    for b in range(B):
        m = {"x": np.ascontiguousarray(np.asarray(inputs["x"][b], np.float32).T),
             "smalls": host_smalls(cfg, b, inputs), "consts": consts}
        m.update(shared)
        in_maps.append(m)
    return in_maps


_CACHE = {}


def kernel(**inputs):
    x = np.asarray(inputs["x"])
    B, S, D = x.shape
    cfg = Cfg(S=S, D=D)
    key = (S, D)
    if key not in _CACHE:
        _CACHE[key] = build_program(cfg)
    nc, _ = _CACHE[key]
    in_maps = make_in_maps(cfg, inputs, B)
    res = run_bass_kernel_spmd(nc, in_maps, core_ids=list(range(B)))
    out = np.stack([np.ascontiguousarray(np.asarray(r["out"], np.float32).T) for r in res.results], axis=0)
    return out
```

```python
import numpy as np
import concourse.bass as bass
import concourse.mybir as mybir
from concourse.bass_utils import run_bass_kernel_spmd

F32 = mybir.dt.float32
BF16 = mybir.dt.bfloat16
AF = mybir.ActivationFunctionType
ALU = mybir.AluOpType

EPS = 1e-6
LRU_C = 8.0
SERIALIZE = False


class Buf:
    __slots__ = ("name", "writer", "readers")

    def __init__(self, name):
        self.name = name
        self.writer = None
        self.readers = {}


class _Rec:
    __slots__ = ("kind", "fn", "inc", "sem", "val")

    def __init__(self, kind, fn=None, sem=None, val=None):
        self.kind = kind
        self.fn = fn
        self.inc = False
        self.sem = sem
        self.val = val


class Eng:
    def __init__(self, name, sem, raw_safe):
        self.name = name
        self.sem = sem
        self.recs = []
        self.count = 0
        self.last_op = None
        self.pending = []
        self.waited = {}
        self.raw_safe = raw_safe


class Ticket:
    __slots__ = ("eng", "rec", "val", "sem")

    def __init__(self, eng=None, rec=None, sem=None, val=None):
        self.eng = eng
        self.rec = rec
        self.sem = sem
        self.val = val


class DmaSem:
    def __init__(self, prog, name):
        self.sem = prog.nc.alloc_semaphore(name=name)
        self.count = 0


class Prog:
    def __init__(self, nc):
        self.nc = nc
        self.engs = {}
        self.n_wait = 0
        self.n_ops = 0

    def add_engine(self, name, raw_safe=False):
        sem = self.nc.alloc_semaphore(name="sem_" + name)
        e = Eng(name, sem, raw_safe)
        self.engs[name] = e
        return e

    def _resolve(self, t):
        if t.val is not None:
            return t.sem, t.val
        e = t.eng
        rec = t.rec
        if rec.val is None:
            last = e.last_op
            assert not last.inc, "lazy-inc bookkeeping"
            last.inc = True
            e.count += 1
            for r in e.pending:
                r.val = e.count
            e.pending = []
        t.sem, t.val = e.sem, rec.val
        assert t.val is not None
        return t.sem, t.val

    def _wait(self, eng, t):
        sem, val = self._resolve(t)
        key = id(sem)
        if eng.waited.get(key, 0) >= val:
            return
        eng.waited[key] = val
        eng.recs.append(_Rec("wait", sem=sem, val=val))
        self.n_wait += 1

    def op(self, eng, fn, reads=(), writes=(), dma_sem=None, ndma=1, eager=True):
        deps = []
        for b in reads:
            if b.writer is not None:
                deps.append(("raw", b.writer))
        for b in writes:
            for r in b.readers.values():
                deps.append(("war", r))
            if b.writer is not None:
                deps.append(("waw", b.writer))
        for kind, t in deps:
            if t.eng is eng and dma_sem is None:
                if eng.raw_safe:
                    continue
            self._wait(eng, t)
        rec = _Rec("op", fn=fn)
        eng.recs.append(rec)
        self.n_ops += 1
        if dma_sem is not None:
            dma_sem.count += 16 * ndma
            rec.sem = dma_sem.sem
            rec.kind = "dma"
            tk = Ticket(sem=dma_sem.sem, val=dma_sem.count)
            key = id(dma_sem)
        else:
            eng.pending.append(rec)
            eng.last_op = rec
            tk = Ticket(eng=eng, rec=rec)
            key = id(eng)
            if eager:
                rec.inc = True
                eng.count += 1
                for r in eng.pending:
                    r.val = eng.count
                eng.pending = []
        for b in reads:
            b.readers[key] = tk
        for b in writes:
            b.writer = tk
            b.readers = {}
        if SERIALIZE:
            for e2 in self.engs.values():
                if e2 is not eng or dma_sem is not None:
                    self._wait(e2, tk)
        return tk

    def wait_ticket(self, eng, t):
        self._wait(eng, t)

    def barrier(self, engs):
        tks = []
        for e in engs:
            if e.last_op is not None:
                tks.append(Ticket(eng=e, rec=e.last_op))
        for e in engs:
            for t in tks:
                if t.eng is not e:
                    self._wait(e, t)

    def emit(self):
        nc = self.nc
        with nc.Block() as block:
            for name, e in self.engs.items():
                def body(h, e=e):
                    for r in e.recs:
                        if r.kind == "wait":
                            h.wait_ge(r.sem, r.val)
                        elif r.kind == "dma":
                            for ins in r.fn(h):
                                ins.then_inc(r.sem, 16)
                        else:
                            ins = r.fn(h)
                            if r.inc:
                                ins.then_inc(e.sem, 1)
                getattr(block, name)(body)


class Cfg:
    def __init__(self, S=2048, D=1024, H=16, DFF=2816, DRNN=1280, DEPTH=2, GH=8):
        self.S, self.D, self.H, self.DFF, self.DRNN, self.DEPTH = S, D, H, DFF, DRNN, DEPTH
        self.KD = D // 128
        self.TC = 512
        self.NT = S // 512
        self.NB = S // 128
        self.KF = DFF // 128
        self.KR = DRNN // 128
        self.MODC = 9 * D
        self.MW = 512 if (9 * D) % 512 == 0 else 256
        self.GH = GH
        self.NSB = (DEPTH + 1) // 2
        self.NL = DEPTH // 2
        self.SLOT = 4096
        self.NSLOT = 3
        assert S % 512 == 0 and D % 256 == 0 and DFF % 256 == 0 and DRNN % 256 == 0
        assert H * 64 == D
        KD, KR = self.KD, self.KR
        o = 0
        self.o_c = o; o += KD
        self.o_modb = o; o += DEPTH * 9 * KD
        self.o_g = o; o += DEPTH * 3 * KD
        self.o_fg = o; o += KD
        self.o_lru = o
        self.lru_stride = KR * 4 + 4 * KR
        o += max(self.NL, 1) * self.lru_stride
        self.NSM = o
        self.NCONST = 5 * 128


def host_consts():
    j = np.arange(128)[:, None]
    s = np.arange(128)[None, :]
    ident = (j == s).astype(np.float32)
    ones = np.ones((128, 128), np.float32)
    negUI = -(j >= s).astype(np.float32)
    negLs = -(j < s).astype(np.float32)
    tri = (j < s).astype(np.float32)
    return np.ascontiguousarray(np.concatenate([ident, ones, negUI, negLs, tri], axis=1))


def colmajor(v):
    v = np.asarray(v, np.float32).reshape(-1, 128)
    return np.ascontiguousarray(v.T)


def host_smalls(cfg, b, inputs):
    sm = np.zeros((128, cfg.NSM), np.float32)
    KD, KR = cfg.KD, cfg.KR
    sm[:, cfg.o_c:cfg.o_c + KD] = colmajor(inputs["c"][b])
    for l in range(cfg.DEPTH):
        sm[:, cfg.o_modb + l * 9 * KD: cfg.o_modb + (l + 1) * 9 * KD] = colmajor(inputs["mod_b"][l])
        sm[:, cfg.o_g + l * 3 * KD: cfg.o_g + (l + 1) * 3 * KD] = colmajor(inputs["norm_g"][l].reshape(-1))
    sm[:, cfg.o_fg:cfg.o_fg + KD] = colmajor(inputs["final_norm_g"])
    for jl in range(cfg.NL):
        o = cfg.o_lru + jl * cfg.lru_stride
        cw = np.asarray(inputs["lru_conv_w"][jl], np.float32)
        sm[:, o:o + KR * 4] = cw.T.reshape(KR, 128, 4).transpose(1, 0, 2).reshape(128, KR * 4)
        o += KR * 4
        for nm in ("lru_conv_b", "lru_b_r", "lru_b_i", "lru_lambda"):
            sm[:, o:o + KR] = colmajor(inputs[nm][jl])
            o += KR
    return sm


def build(cfg, stages=None, final=True, plan=None):
    planning = plan is None
    S, D, KD, NT, NB, KF, KR, DFF, DRNN = cfg.S, cfg.D, cfg.KD, cfg.NT, cfg.NB, cfg.KF, cfg.KR, cfg.DFF, cfg.DRNN
    DEPTH = cfg.DEPTH
    if stages is None:
        stages = [(l, s) for l in range(DEPTH) for s in range(3)]
    nc = bass.Bass("TRN2", target_bir_lowering=False)
    T = {}

    def din(name, shape):
        T[name] = nc.dram_tensor(name, list(shape), F32, kind="ExternalInput").ap()

    din("x", [D, S])
    din("smalls", [128, cfg.NSM])
    din("consts", [128, cfg.NCONST])
    din("mod_w", [DEPTH, D, 9 * D])
    din("ffn_w_gu", [DEPTH, 2, D, 2 * DFF])
    din("ffn_w_down", [DEPTH, 2, DFF, D])
    din("sb_w_qkv", [cfg.NSB, D, 3 * D])
    din("sb_w_o", [cfg.NSB, D, D])
    din("lru_w_in", [max(cfg.NL, 1), D, 2 * DRNN])
    din("lru_w_r", [max(cfg.NL, 1), KR, 128, 128])
    din("lru_w_i", [max(cfg.NL, 1), KR, 128, 128])
    din("lru_w_out", [max(cfg.NL, 1), DRNN, D])
    out_ap = nc.dram_tensor("out", [D, S], F32, kind="ExternalOutput").ap()

    P = Prog(nc)
    pe = P.add_engine("tensor", raw_safe=True)
    act = P.add_engine("scalar")
    dve = P.add_engine("vector")
    pool = P.add_engine("gpsimd")
    sp = P.add_engine("sync")
    compute_engs = [pe, act, dve, pool]

    def OP(eng, fn, reads=(), writes=(), **kw):
        if planning:
            return None
        return P.op(eng, fn, reads, writes, **kw)

    def sb(name, shape, dt):
        return nc.alloc_sbuf_tensor(name, list(shape), dt)

    xT = sb("xT", [128, KD, S], F32)
    hT = sb("hT", [128, KD, S], BF16)
    xb = [Buf(f"x{t}") for t in range(NT)]
    hb = [Buf(f"h{t}") for t in range(NT)]
    sm = sb("sm", [128, cfg.NSM], F32)
    smb = Buf("sm")
    ident = sb("ident", [128, 128], F32)
    cbf = sb("cbf", [128, 3 * 128], BF16)
    tri = sb("tri", [128, 128], F32)
    cb = Buf("consts")
    ones_bf = cbf[:, 0:128]
    negUI = cbf[:, 128:256]
    negLs = cbf[:, 256:384]
    zeros_bf = sb("zeros", [128, 128], BF16)
    modv = [sb(f"modv{i}", [128, 9 * KD], F32) for i in range(2)]
    Asc = [sb(f"Asc{i}", [128, 3 * KD], F32) for i in range(2)]
    gsc = [sb(f"gsc{i}", [128, 3 * KD], F32) for i in range(2)]
    modb = [Buf("modv0"), Buf("modv1")]
    fgb = Buf("fgs")
    cact = sb("cact", [128, KD], BF16)
    cactb = Buf("cact")
    fgs = sb("fgs", [128, KD], F32)
    lruc = sb("lruc", [128, KR], F32)
    lrub = Buf("lruc")
    sqk = [sb(f"sqk{i}", [128, 512], BF16) for i in range(4)]
    sqkb = [Buf(f"sqk{i}") for i in range(4)]
    invs = [sb(f"inv{i}", [128, 512], F32) for i in range(2)]
    invsb = [Buf(f"inv{i}") for i in range(2)]
    ntmp = [sb(f"ntmp{i}", [128, 512], F32) for i in range(3)]
    ntmpb = [Buf(f"ntmp{i}") for i in range(3)]
    xstageb = [Buf(f"xstage{i}") for i in range(2)]
    c_sem = DmaSem(P, "csem")
    wslots = [sb(f"wslot{i}", [128, cfg.SLOT], BF16) for i in range(cfg.NSLOT)]
    wbufs = [Buf(f"wslot{i}") for i in range(cfg.NSLOT)]
    wsems = [DmaSem(P, f"wsem{i}") for i in range(cfg.NSLOT)]
    banks = [nc.alloc_psum_tensor(f"bank{i}", [128, 512], F32) for i in range(8)]
    bb = [Buf(f"bank{i}") for i in range(8)]

    units = [] if planning else plan
    wstate = {"consumed": 0, "issued": 0}

    def w_issue(i):
        slot = i % cfg.NSLOT
        tile = wslots[slot]
        pieces = units[i]

        def fn(h, tile=tile, pieces=pieces):
            res = []
            for (src, off, a, b) in pieces:
                dst = tile[:, off:off + a * b].rearrange("p (a b) -> p a b", a=a)
                res.append(h.dma_start(out=dst, in_=src(T)))
            return res
        P.op(pool, fn, writes=[wbufs[slot]], dma_sem=wsems[slot], ndma=len(pieces))

    def w_get(pieces):
        i = wstate["consumed"]
        wstate["consumed"] += 1
        if planning:
            units.append(pieces)
            return wslots[0], wbufs[0]
        while wstate["issued"] < min(len(units), i + cfg.NSLOT):
            w_issue(wstate["issued"])
            wstate["issued"] += 1
        slot = i % cfg.NSLOT
        return wslots[slot], wbufs[slot]

    def wview(tile, off, a, b):
        return tile[:, off:off + a * b].rearrange("p (a b) -> p a b", a=a)

    def MM(out, lhsT, rhs, start, stop, reads, writes, skip=False):
        if skip:
            OP(pe, lambda h: h.matmul(out, lhsT=lhsT, rhs=rhs, start=start, stop=stop, skip_group_check=True),
               reads, writes, eager=bool(stop))
        else:
            OP(pe, lambda h: h.matmul(out, lhsT=lhsT, rhs=rhs, start=start, stop=stop), reads, writes,
               eager=bool(stop))

    def ACT(out, in_, func, reads, writes, bias=None, scale=None):
        kw = {}
        if bias is not None:
            kw["bias"] = bias
        if scale is not None:
            kw["scale"] = scale
        return OP(act, lambda h: h.activation(out=out, in_=in_, func=func, **kw), reads, writes)

    def TT(eng, out, in0, in1, op, reads, writes):
        return OP(eng, lambda h: h.tensor_tensor(out=out, in0=in0, in1=in1, op=op), reads, writes)

    def TS(eng, out, in0, s1, s2, op0, op1, reads, writes):
        if op1 is None:
            return OP(eng, lambda h: h.tensor_scalar(out=out, in0=in0, scalar1=s1, scalar2=None, op0=op0),
                      reads, writes)
        return OP(eng, lambda h: h.tensor_scalar(out=out, in0=in0, scalar1=s1, scalar2=s2, op0=op0, op1=op1),
                  reads, writes)

    def STT(eng, out, in0, scalar, in1, op0, op1, reads, writes):
        return OP(eng, lambda h: h.scalar_tensor_tensor(out=out, in0=in0, scalar=scalar, in1=in1, op0=op0, op1=op1),
                  reads, writes)

    def CP(eng, out, in_, reads, writes):
        return OP(eng, lambda h: h.tensor_copy(out=out, in_=in_), reads, writes)

    def phase_barrier():
        if not planning:
            P.barrier(compute_engs)

    rot = {"gu": 0, "dn": 0, "nt": 0, "ev": 0, "nm": 0, "sq": 0, "nv": 0}

    def tl(name, shape, dt):
        rot["nm"] += 1
        return nc.sbuf_tensor(f"{name}_{rot['nm']}", list(shape), dt)

    def prologue():
        OP(sp, lambda h: [h.dma_start(out=sm[:], in_=T["smalls"][:, :]),
                          h.dma_start(out=ident[:], in_=T["consts"][:, 0:128]),
                          h.dma_start(out=tri[:], in_=T["consts"][:, 512:640])],
           writes=[smb, cb], dma_sem=c_sem, ndma=3)
        c2 = DmaSem(P, "csem2")
        OP(pool, lambda h: [h.dma_start(out=cbf[:], in_=T["consts"][:, 128:512])],
           writes=[cb], dma_sem=c2, ndma=1)
        OP(dve, lambda h: h.memset(zeros_bf[:], 0.0), writes=[cb])
        ACT(cact[:], sm[:, cfg.o_c:cfg.o_c + KD], AF.Silu, [smb], [cactb])
        x_load_chunks(range(0, 1))
        if stages:
            cur["l"] = stages[0][0]
            mod_require(stages[0][0], stages[0][1], 7)
            gate = wbufs[(wstate["consumed"] - 1) % cfg.NSLOT].writer if not planning else None
            if gate is not None:
                P.wait_ticket(sp, gate)
        x_load_chunks(range(1, NT))

    xload = {"done": NB, "last": None}
    xsems = [DmaSem(P, f"xs{i}") for i in range(NT)]

    def x_load_chunks(tcns):
        xsrc = T["x"].rearrange("(k p) s -> p k s", p=128)
        for tcn in tcns:
            cs = slice(tcn * 512, (tcn + 1) * 512)
            OP(sp, (lambda h, cs=cs: [h.dma_start(out=xT[:, :, cs], in_=xsrc[:, :, cs])]),
               writes=[xb[tcn]], dma_sem=xsems[tcn], ndma=1)

    mod_pending = []
    mod_done = set()
    mod_fin = set()
    MWc = cfg.MW
    nfl_ = MWc // 128
    NMU = cfg.MODC // MWc

    def mod_enqueue(l):
        for u in range(NMU):
            if (l, u) not in mod_done and (l, u) not in mod_pending:
                mod_pending.append((l, u))

    def mod_unit(l, u, bank_i):
        mod_done.add((l, u))
        pieces = [((lambda T, l=l, u=u: T["mod_w"][l].rearrange("(k p) c -> p k c", p=128)[:, :, u * MWc:(u + 1) * MWc]),
                   0, KD, MWc)]
        wt, wb = w_get(pieces)
        wv = wview(wt, 0, KD, MWc)
        mbank = banks[bank_i]
        for fl in range(nfl_):
            for k in range(KD):
                MM(mbank[:, fl:fl + 1], wv[:, k, fl * 128:(fl + 1) * 128], cact[:, k:k + 1],
                   k == 0, k == KD - 1, [wb, cactb], [bb[bank_i]])
        ob = cfg.o_modb + l * 9 * KD + u * nfl_
        TT(dve, modv[l % 2][:, u * nfl_:(u + 1) * nfl_], mbank[:, 0:nfl_], sm[:, ob:ob + nfl_], ALU.add,
           [bb[bank_i], smb], [modb[l % 2]])

    def mod_pump(n, bank_i):
        for _ in range(n):
            if not mod_pending:
                return
            l, u = mod_pending.pop(0)
            mod_unit(l, u, bank_i)

    def mod_require(l, sub, bank_i):
        u_lo = (sub * 3 * D) // MWc
        u_hi = ((sub + 1) * 3 * D + MWc - 1) // MWc
        for u in range(u_lo, u_hi):
            if (l, u) not in mod_done:
                if (l, u) in mod_pending:
                    mod_pending.remove((l, u))
                mod_unit(l, u, bank_i)
        if (l, sub) in mod_fin:
            return
        mod_fin.add((l, sub))
        mv, mb_ = modv[l % 2], modb[l % 2]
        og = cfg.o_g + l * 3 * KD
        sc = mv[:, (sub * 3 + 1) * KD:(sub * 3 + 2) * KD]
        gt = mv[:, (sub * 3 + 2) * KD:(sub * 3 + 3) * KD]
        A = Asc[l % 2][:, sub * KD:(sub + 1) * KD]
        STT(dve, A, sc, 1.0, sm[:, og + sub * KD: og + (sub + 1) * KD], ALU.add, ALU.mult, [mb_, smb], [mb_])
        TS(dve, A, A, float(np.sqrt(D)), None, ALU.mult, None, [mb_], [mb_])
        mw = 1.0 if sub == 1 else 0.5
        TS(dve, gsc[l % 2][:, sub * KD:(sub + 1) * KD], gt, 1.0, mw, ALU.add, ALU.mult, [mb_], [mb_])

    def norm_stats(tcn):
        cs = slice(tcn * 512, (tcn + 1) * 512)
        for k in range(KD):
            r = rot["sq"] % 4
            rot["sq"] += 1
            ACT(sqk[r][:], xT[:, k, cs], AF.Square, [xb[tcn]], [sqkb[r]])
            MM(banks[7][:], ones_bf, sqk[r][:], k == 0, k == KD - 1, [sqkb[r], cb], [bb[7]])
        iv = rot["nv"] % 2
        rot["nv"] += 1
        ACT(invs[iv][:], banks[7][:], AF.Ln, [bb[7]], [invsb[iv]], bias=float(D * EPS), scale=1.0)
        ACT(invs[iv][:], invs[iv][:], AF.Exp, [invsb[iv]], [invsb[iv]], scale=-0.5)
        return iv

    def norm_apply(tcn, iv, Acols, shiftcols, dst_fn, dstbufs, areads):
        cs = slice(tcn * 512, (tcn + 1) * 512)
        if tcn == NT - 1 and xload["last"] is not None:
            if not planning:
                P.wait_ticket(act, xload["last"])
                P.wait_ticket(dve, xload["last"])
            xload["last"] = None
        for k in range(KD):
            if shiftcols is None:
                STT(dve, dst_fn(k, tcn), xT[:, k, cs], Acols[:, k:k + 1], invs[iv][:], ALU.mult, ALU.mult,
                    [xb[tcn], invsb[iv]] + areads, [dstbufs[tcn]])
                continue
            i = rot["nt"] % 3
            rot["nt"] += 1
            STT(dve, ntmp[i][:], xT[:, k, cs], Acols[:, k:k + 1], invs[iv][:], ALU.mult, ALU.mult,
                [xb[tcn], invsb[iv]] + areads, [ntmpb[i]])
            if k % 2 == 0:
                ACT(dst_fn(k, tcn), ntmp[i][:], AF.Identity, [ntmpb[i]] + areads, [dstbufs[tcn]],
                    bias=shiftcols[:, k:k + 1], scale=1.0)
            else:
                TS(dve, dst_fn(k, tcn), ntmp[i][:], shiftcols[:, k:k + 1], None, ALU.add, None,
                   [ntmpb[i]] + areads, [dstbufs[tcn]])

    def norm_phase(Acols, shiftcols, dst_fn, dstbufs, areads, after_apply=None):
        iv = norm_stats(0)
        for tcn in range(NT):
            iv_next = norm_stats(tcn + 1) if tcn + 1 < NT else None
            norm_apply(tcn, iv, Acols, shiftcols, dst_fn, dstbufs, areads)
            if after_apply is not None:
                after_apply(tcn)
            iv = iv_next

    cur = {"l": 0}

    def sub_norm(sub, after_apply=None):
        l = cur["l"]
        mod_require(l, sub, 7)
        norm_phase(Asc[l % 2][:, sub * KD:(sub + 1) * KD], modv[l % 2][:, (sub * 3) * KD:(sub * 3 + 1) * KD],
                   lambda k, tcn: hT[:, k, tcn * 512:(tcn + 1) * 512], hb, [modb[l % 2]], after_apply=after_apply)

    def resid_update(bank_i, o, tcn, sub):
        l = cur["l"]
        cs = slice(tcn * 512, (tcn + 1) * 512)
        STT(dve, xT[:, o, cs], banks[bank_i][:], gsc[l % 2][:, sub * KD + o: sub * KD + o + 1], xT[:, o, cs],
            ALU.mult, ALU.add, [bb[bank_i], xb[tcn], modb[l % 2]], [xb[tcn]])

    def ffn_phase(l, f, sub):
        mod_require(l, sub, 7)
        GH = cfg.GH
        groups = []
        g0 = 0
        while g0 < KF:
            gn = min(GH, KF - g0)
            groups.append((g0, gn))
            g0 += gn
        with tl("actT", [128, GH, S], BF16) as actT, \
                tl("sg0", [128, 512], F32) as sg0, tl("sg1", [128, 512], F32) as sg1:
            sgs = [sg0, sg1]
            sgb = [Buf("sg0"), Buf("sg1")]
            actb = [Buf(f"act{t}") for t in range(NT)]

            def gu_get(g0, pu):
                c0 = (g0 + 2 * pu) * 128
                pieces = [
                    ((lambda T, l=l, f=f, c0=c0: T["ffn_w_gu"][l, f].rearrange("(k p) c -> p k c", p=128)[:, :, c0:c0 + 256]),
                     0, KD, 256),
                    ((lambda T, l=l, f=f, c0=c0: T["ffn_w_gu"][l, f].rearrange("(k p) c -> p k c", p=128)[:, :, DFF + c0:DFF + c0 + 256]),
                     KD * 256, KD, 256)]
                wt, wb = w_get(pieces)
                return wview(wt, 0, KD, 256), wview(wt, KD * 256, KD, 256), wb

            def gu_emit(wg, wu, wb, pu, cc, tcn):
                fa = 2 * pu + cc
                cs = slice(tcn * 512, (tcn + 1) * 512)
                r = rot["gu"] % 2
                rot["gu"] += 1
                gi, ui = r, 2 + r
                for k in range(KD):
                    MM(banks[gi][:], wg[:, k, cc * 128:(cc + 1) * 128], hT[:, k, cs], k == 0, k == KD - 1,
                       [wb, hb[tcn]], [bb[gi]])
                for k in range(KD):
                    MM(banks[ui][:], wu[:, k, cc * 128:(cc + 1) * 128], hT[:, k, cs], k == 0, k == KD - 1,
                       [wb, hb[tcn]], [bb[ui]])
                ACT(sgs[r][:], banks[gi][:], AF.Silu, [bb[gi]], [sgb[r]])
                TT(dve, actT[:, fa, cs], banks[ui][:], sgs[r][:], ALU.mult, [bb[ui], sgb[r]], [actb[tcn]])

            wg0, wu0, wb0 = gu_get(groups[0][0], 0)

            def first_unit(tcn):
                for cc in range(2):
                    gu_emit(wg0, wu0, wb0, 0, cc, tcn)
            sub_norm(sub, after_apply=first_unit)
            mod_pump(1, 7)
            for gidx, (g0, gn) in enumerate(groups):
                assert gn % 2 == 0
                for pu in range(gn // 2):
                    if gidx == 0 and pu == 0:
                        continue
                    wg, wu, wb = gu_get(g0, pu)
                    for cc in range(2):
                        for tcn in range(NT):
                            gu_emit(wg, wu, wb, pu, cc, tcn)
                    mod_pump(1, 7)
                for du in range(D // 256):
                    pieces = [((lambda T, l=l, f=f, g0=g0, gn=gn, du=du:
                                T["ffn_w_down"][l, f][g0 * 128:(g0 + gn) * 128, :].rearrange("(k p) c -> p k c", p=128)[:, :, du * 256:(du + 1) * 256]),
                               0, gn, 256)]
                    wt, wb = w_get(pieces)
                    wd = wview(wt, 0, gn, 256)
                    for oc in range(2):
                        o = du * 2 + oc
                        for tcn in range(NT):
                            cs = slice(tcn * 512, (tcn + 1) * 512)
                            bi = 4 + (rot["dn"] % 2)
                            rot["dn"] += 1
                            for k in range(gn):
                                MM(banks[bi][:], wd[:, k, oc * 128:(oc + 1) * 128], actT[:, k, cs], k == 0, k == gn - 1,
                                   [wb, actb[tcn]], [bb[bi]])
                            resid_update(bi, o, tcn, sub)
                    mod_pump(1, 7)
            phase_barrier()

    def attn_phase(l, j):
        sub = 1
        mod_require(cur["l"], sub, 7)
        with tl("qT", [128, 2, S], BF16) as qT, tl("kT", [128, 2, S], BF16) as kT, \
                tl("vv", [128, NB, 256], BF16) as vv, tl("oT", [128, 2, S], BF16) as oT, \
                tl("e_sb", [128, 4, 512], F32) as e_sb, tl("sp_sb", [128, 4, 512], BF16) as sp_sb, \
                tl("E_sb", [128, 4, 512], F32) as E_sb, tl("w_sb", [128, 4, 512], BF16) as w_sb:
            qb_, kb_, vb_, ob_ = Buf("qT"), Buf("kT"), Buf("vv"), Buf("oT")
            eb = [Buf(f"e{i}") for i in range(4)]
            spb = [Buf(f"sp{i}") for i in range(4)]
            Eb = [Buf(f"E{i}") for i in range(4)]
            wb_ = [Buf(f"w{i}") for i in range(4)]
            pj = {"i": 0}

            def pbank():
                pj["i"] += 1
                return pj["i"] % 2
            def qk_get(hg):
                c0 = hg * 256
                pieces = [
                    ((lambda T, j=j, c0=c0: T["sb_w_qkv"][j].rearrange("(k p) c -> p k c", p=128)[:, :, c0:c0 + 256]), 0, KD, 256),
                    ((lambda T, j=j, c0=c0: T["sb_w_qkv"][j].rearrange("(k p) c -> p k c", p=128)[:, :, D + c0:D + c0 + 256]), KD * 256, KD, 256)]
                wt, wb = w_get(pieces)
                return wview(wt, 0, KD, 256), wview(wt, KD * 256, KD, 256), wb

            def qk_emit(wq, wk, wb, hp2, tcn):
                cs = slice(tcn * 512, (tcn + 1) * 512)
                bi = pbank()
                for k in range(KD):
                    MM(banks[bi][:], wq[:, k, hp2 * 128:(hp2 + 1) * 128], hT[:, k, cs], k == 0, k == KD - 1,
                       [wb, hb[tcn]], [bb[bi]])
                ACT(qT[:, hp2, cs], banks[bi][:], AF.Identity, [bb[bi]], [qb_], bias=0.0, scale=0.125)
                bi = pbank()
                for k in range(KD):
                    MM(banks[bi][:], wk[:, k, hp2 * 128:(hp2 + 1) * 128], hT[:, k, cs], k == 0, k == KD - 1,
                       [wb, hb[tcn]], [bb[bi]])
                CP(dve, kT[:, hp2, cs], banks[bi][:], [bb[bi]], [kb_])

            wq0, wk0, wb0 = qk_get(0)

            def first_qk(tcn):
                for hp2 in range(2):
                    qk_emit(wq0, wk0, wb0, hp2, tcn)
            sub_norm(sub, after_apply=first_qk)
            for hg in range(D // 256):
                c0 = hg * 256
                if hg > 0:
                    wq, wk, wb = qk_get(hg)
                    for hp2 in range(2):
                        for tcn in range(NT):
                            qk_emit(wq, wk, wb, hp2, tcn)
                pieces = [((lambda T, j=j, c0=c0: T["sb_w_qkv"][j].rearrange("(k p) c -> p k c", p=128)[:, :, 2 * D + c0:2 * D + c0 + 256]), 0, KD, 256)]
                wt, wb = w_get(pieces)
                wvv = wview(wt, 0, KD, 256)
                for tb in range(NB):
                    bi = pbank()
                    for k in range(KD):
                        MM(banks[bi][:, 0:256], hT[:, k, tb * 128:(tb + 1) * 128], wvv[:, k, :], k == 0, k == KD - 1,
                           [wb, hb[tb // 4]], [bb[bi]])
                    if tb % 2:
                        CP(dve, vv[:, tb, :], banks[bi][:, 0:256], [bb[bi]], [vb_])
                    else:
                        ACT(vv[:, tb, :], banks[bi][:, 0:256], AF.Copy, [bb[bi]], [vb_])
                pieces = [((lambda T, j=j, c0=c0: T["sb_w_o"][j][c0:c0 + 256, :].rearrange("(k p) c -> p k c", p=128)), 0, 2, D)]
                wot, wob = w_get(pieces)
                wo = wview(wot, 0, 2, D)
                prs = [slice(0, 64), slice(64, 128)]
                for qc in range(NT):
                    q0 = qc * 512
                    for bi in (2, 3, 4, 5, 6, 7):
                        MM(banks[bi][:], zeros_bf[:, :], hT[:, 0, 0:512], True, True, [cb, hb[0]], [bb[bi]])
                    kbs = list(range(4 * qc + 3, -1, -1))

                    def geom(kb, qc=qc):
                        jd = kb - 4 * qc
                        cst = 128 * jd if jd > 0 else 0
                        return jd, cst

                    def zmm(s_, kb, q0=q0):
                        jd, cst = geom(kb)
                        hp2, pr = s_ // 2, prs[s_ % 2]
                        zb = s_ % 2
                        MM(banks[zb][:, cst:512], kT[pr, hp2, kb * 128:(kb + 1) * 128], qT[pr, hp2, q0 + cst:q0 + 512],
                           True, True, [kb_, qb_], [bb[zb]])

                    def e_op(s_, kb):
                        jd, cst = geom(kb)
                        zb = s_ % 2
                        ACT(e_sb[:, s_, cst:512], banks[zb][:, cst:512], AF.Exp, [bb[zb]], [eb[s_]])
                        if jd >= 0:
                            TT(dve, e_sb[:, s_, cst:cst + 128], e_sb[:, s_, cst:cst + 128], tri[:, :], ALU.mult,
                               [eb[s_], cb], [eb[s_]])

                    def sp_op(s_, kb):
                        jd, cst = geom(kb)
                        ACT(sp_sb[:, s_, cst:512], e_sb[:, s_, cst:512], AF.Ln, [eb[s_]], [spb[s_]], bias=1.0, scale=1.0)

                    def r1_op(s_, kb):
                        jd, cst = geom(kb)
                        MM(banks[2 + s_][:, cst:512], negUI, sp_sb[:, s_, cst:512], False, True, [cb, spb[s_]], [bb[2 + s_]], skip=True)

                    def E_op(s_, kb):
                        jd, cst = geom(kb)
                        ACT(E_sb[:, s_, cst:512], banks[2 + s_][:, cst:512], AF.Exp, [bb[2 + s_]], [Eb[s_]])

                    def w_op(s_, kb):
                        jd, cst = geom(kb)
                        TT(dve, w_sb[:, s_, cst:512], e_sb[:, s_, cst:512], E_sb[:, s_, cst:512], ALU.mult,
                           [eb[s_], Eb[s_]], [wb_[s_]])

                    def o_op(s_, kb):
                        jd, cst = geom(kb)
                        hp2, hd = s_ // 2, s_ % 2
                        MM(banks[6 + hp2][hd * 64:(hd + 1) * 64, cst:512], vv[:, kb, hp2 * 128 + hd * 64: hp2 * 128 + (hd + 1) * 64],
                           w_sb[:, s_, cst:512], False, True, [vb_, wb_[s_]], [bb[6 + hp2]], skip=True)

                    def r2_op(s_, kb):
                        jd, cst = geom(kb)
                        MM(banks[2 + s_][:, cst:512], negLs, sp_sb[:, s_, cst:512], False, True, [cb, spb[s_]], [bb[2 + s_]], skip=True)

                    zmm(0, kbs[0])
                    zmm(1, kbs[0])
                    for ii, kb in enumerate(kbs):
                        last = ii + 1 == len(kbs)
                        e_op(0, kb)
                        e_op(1, kb)
                        zmm(2, kb)
                        zmm(3, kb)
                        sp_op(0, kb)
                        sp_op(1, kb)
                        r1_op(0, kb)
                        r1_op(1, kb)
                        e_op(2, kb)
                        e_op(3, kb)
                        if not last:
                            zmm(0, kbs[ii + 1])
                            zmm(1, kbs[ii + 1])
                        sp_op(2, kb)
                        sp_op(3, kb)
                        r1_op(2, kb)
                        r1_op(3, kb)
                        for s_ in range(4):
                            E_op(s_, kb)
                        for s_ in range(4):
                            w_op(s_, kb)
                        for pair in (0, 2):
                            o_op(pair, kb)
                            o_op(pair + 1, kb)
                            if not last:
                                r2_op(pair, kb)
                                r2_op(pair + 1, kb)
                    ACT(oT[:, 0, q0:q0 + 512], banks[6][:, :], AF.Copy, [bb[6]], [ob_])
                    CP(dve, oT[:, 1, q0:q0 + 512], banks[7][:, :], [bb[7]], [ob_])
                for o in range(KD):
                    for tcn in range(NT):
                        cs = slice(tcn * 512, (tcn + 1) * 512)
                        bi = pbank()
                        for k in range(2):
                            MM(banks[bi][:], wo[:, k, o * 128:(o + 1) * 128], oT[:, k, cs], k == 0, k == 1, [wob, ob_], [bb[bi]])
                        resid_update(bi, o, tcn, sub)
                mod_pump(5, 0)
            phase_barrier()

    def lru_phase(l, j):
        sub = 1
        mod_require(cur["l"], sub, 7)
        ol = cfg.o_lru + j * cfg.lru_stride
        o_cw = ol
        o_cb = ol + KR * 4
        o_br = o_cb + KR
        o_bi = o_br + KR
        o_lam = o_bi + KR
        NH = 2
        TH = S // NH
        NTH = TH // 512
        with tl("xbp0", [128, TH + 4], F32) as xbp0, tl("xbp1", [128, TH + 4], F32) as xbp1, \
                tl("xc0", [128, TH], F32) as xc0, tl("xc1", [128, TH], F32) as xc1, \
                tl("xcb0", [128, TH], BF16) as xcb0, tl("xcb1", [128, TH], BF16) as xcb1, \
                tl("ra0", [128, TH], F32) as ra0, tl("ra1", [128, TH], F32) as ra1, \
                tl("ib0", [128, TH], F32) as ib0, tl("ib1", [128, TH], F32) as ib1, \
                tl("yT", [128, 4, TH], BF16) as yT, tl("wri", [128, 2 * KR * 128], BF16) as wri, \
                tl("lt", [128, KR], F32) as lt, tl("halo", [128, KR, 4], F32) as halo, \
                tl("hlast", [128, KR], F32) as hlast, \
                tl("hsB0", [128, TH], F32) as hsB0, tl("hsB1", [128, TH], F32) as hsB1, \
                tl("ggB0", [128, TH], F32) as ggB0, tl("ggB1", [128, TH], F32) as ggB1:
            hsB = [hsB0, hsB1]
            ggB = [ggB0, ggB1]
            Bh = [Buf("hsB0"), Buf("hsB1")]
            Bg = [Buf("ggB0"), Buf("ggB1")]
            xbp = [xbp0, xbp1]
            xc = [xc0, xc1]
            xcb = [xcb0, xcb1]
            ra = [ra0, ra1]
            ib = [ib0, ib1]
            t2 = [xbp0[:, 4:4 + TH], xbp1[:, 4:4 + TH]]
            B_ = {n: Buf(n) for n in ["yT", "wri", "lt", "halo", "hlast"]}
            Bx = [Buf("xbp0"), Buf("xbp1")]
            Bc = [Buf("xc0"), Buf("xc1")]
            Bcb = [Buf("xcb0"), Buf("xcb1")]
            Br = [Buf("ra0"), Buf("ra1")]
            Bi = [Buf("ib0"), Buf("ib1")]
            cengs = [dve, dve]
            ACT(lt[:], sm[:, o_lam:o_lam + KR], AF.Exp, [smb], [B_["lt"]], scale=-1.0)
            ACT(lt[:], lt[:], AF.Ln, [B_["lt"]], [B_["lt"]], bias=1.0, scale=1.0)
            TS(dve, lruc[:], lt[:], -LRU_C, None, ALU.mult, None, [B_["lt"]], [lrub])
            pieces = [((lambda T, j=j: T["lru_w_r"][j].rearrange("n k c -> k n c")), 0, KR, 128),
                      ((lambda T, j=j: T["lru_w_i"][j].rearrange("n k c -> k n c")), KR * 128, KR, 128)]
            wt, wb = w_get(pieces)
            CP(dve, wri[:, :], wt[:, 0:2 * KR * 128], [wb], [B_["wri"]])
            wr = wri[:, 0:KR * 128].rearrange("p (a b) -> p a b", a=KR)
            wi = wri[:, KR * 128:2 * KR * 128].rearrange("p (a b) -> p a b", a=KR)
            OP(dve, lambda h: h.memset(halo[:], 0.0), writes=[B_["halo"]])
            OP(dve, lambda h: h.memset(hlast[:], 0.0), writes=[B_["hlast"]])
            its = [(half, ug) for half in range(NH) for ug in range(KR // 2)]

            def A1(it, defer=False):
                half, ug = it
                t0 = half * TH
                c0 = ug * 256
                ns = [ug * 2, ug * 2 + 1]
                pieces = [((lambda T, j=j, c0=c0: T["lru_w_in"][j].rearrange("(k p) c -> p k c", p=128)[:, :, DRNN + c0:DRNN + c0 + 256]), 0, KD, 256)]
                wt, wb = w_get(pieces)
                wxb = wview(wt, 0, KD, 256)
                for c in range(2):
                    CP(dve, xbp[c][:, 0:4], halo[:, ns[c], :], [B_["halo"]], [Bx[c]])

                def chunk(tch):
                    for c in range(2):
                        cs = slice(t0 + tch * 512, t0 + (tch + 1) * 512)
                        bi = 6 + (rot["ev"] % 2)
                        rot["ev"] += 1
                        for k in range(KD):
                            MM(banks[bi][:], wxb[:, k, c * 128:(c + 1) * 128], hT[:, k, cs], k == 0, k == KD - 1,
                               [wb, hb[cs.start // 512]], [bb[bi]])
                        ACT(xbp[c][:, 4 + tch * 512: 4 + (tch + 1) * 512], banks[bi][:], AF.Copy, [bb[bi]], [Bx[c]])
                if defer:
                    return chunk
                for tch in range(NTH):
                    chunk(tch)

            def A2a(it):
                half, ug = it
                ns = [ug * 2, ug * 2 + 1]
                for c in range(2):
                    n = ns[c]
                    TS(dve, xc[c][:, :], xbp[c][:, 1:1 + TH], sm[:, o_cw + n * 4: o_cw + n * 4 + 1], sm[:, o_cb + n: o_cb + n + 1],
                       ALU.mult, ALU.add, [Bx[c], smb], [Bc[c]])
                for tap in range(1, 4):
                    for c in range(2):
                        n = ns[c]
                        STT(dve, xc[c][:, :], xbp[c][:, 1 + tap:1 + tap + TH], sm[:, o_cw + n * 4 + tap: o_cw + n * 4 + tap + 1], xc[c][:, :],
                            ALU.mult, ALU.add, [Bx[c], Bc[c], smb], [Bc[c]])
                for c in range(2):
                    ACT(halo[:, ns[c], :], xbp[c][:, TH:TH + 4], AF.Copy, [Bx[c]], [B_["halo"]])
                    ACT(xcb[c][:, :], xc[c][:, :], AF.Copy, [Bc[c]], [Bcb[c]])

            def A2b(it):
                half, ug = it
                ns = [ug * 2, ug * 2 + 1]
                for c in range(2):
                    n = ns[c]
                    for tch in range(NTH):
                        ls = slice(tch * 512, (tch + 1) * 512)
                        bi = 4 + (rot["dn"] % 2)
                        rot["dn"] += 1
                        MM(banks[bi][:], wr[:, n, :], xcb[c][:, ls], True, True, [B_["wri"], Bcb[c]], [bb[bi]])
                        ACT(ra[c][:, ls], banks[bi][:], AF.Sigmoid, [bb[bi], smb], [Br[c]], bias=sm[:, o_br + n:o_br + n + 1], scale=1.0)
                        bi = 4 + (rot["dn"] % 2)
                        rot["dn"] += 1
                        MM(banks[bi][:], wi[:, n, :], xcb[c][:, ls], True, True, [B_["wri"], Bcb[c]], [bb[bi]])
                        ACT(ib[c][:, ls], banks[bi][:], AF.Sigmoid, [bb[bi], smb], [Bi[c]], bias=sm[:, o_bi + n:o_bi + n + 1], scale=1.0)
                for c in range(2):
                    ACT(ra[c][:, :], ra[c][:, :], AF.Exp, [Br[c], lrub], [Br[c]], scale=lruc[:, ns[c]:ns[c] + 1])
                for c in range(2):
                    TT(pool, ib[c][:, :], ib[c][:, :], xc[c][:, :], ALU.mult, [Bi[c], Bc[c]], [Bi[c]])
                    TT(pool, xc[c][:, :], ra[c][:, :], ra[c][:, :], ALU.mult, [Br[c]], [Bc[c]])
                for c in range(2):
                    ACT(xc[c][:, :], xc[c][:, :], AF.Sqrt, [Bc[c]], [Bc[c]], bias=1.0, scale=-1.0)

            def A2c(it):
                for c in range(2):
                    TT(dve, ib[c][:, :], ib[c][:, :], xc[c][:, :], ALU.mult, [Bi[c], Bc[c]], [Bi[c]])

            def A3(it):
                half, ug = it
                ns = [ug * 2, ug * 2 + 1]
                for c in range(2):
                    n = ns[c]
                    OP(dve, (lambda h, c=c, n=n: h.tensor_tensor_scan(out=hsB[c][:, :], data0=ra[c][:, :], data1=ib[c][:, :],
                                                                   initial=hlast[:, n:n + 1], op0=ALU.mult, op1=ALU.add)),
                       reads=[Br[c], Bi[c], B_["hlast"]], writes=[Bh[c]])
                    ACT(hlast[:, n:n + 1], hsB[c][:, TH - 1:TH], AF.Copy, [Bh[c]], [B_["hlast"]])

            def gitems():
                return [(tch, c) for tch in range(NTH) for c in range(2)]

            def B1a(it):
                half, ug = it
                t0 = half * TH
                c0 = ug * 256
                pieces = [((lambda T, j=j, c0=c0: T["lru_w_in"][j].rearrange("(k p) c -> p k c", p=128)[:, :, c0:c0 + 256]), 0, KD, 256)]
                wt, wb = w_get(pieces)
                wga = wview(wt, 0, KD, 256)
                for b_, (tch, c) in enumerate(gitems()):
                    cs = slice(t0 + tch * 512, t0 + (tch + 1) * 512)
                    for k in range(KD):
                        MM(banks[b_][:], wga[:, k, c * 128:(c + 1) * 128], hT[:, k, cs], k == 0, k == KD - 1,
                           [wb, hb[cs.start // 512]], [bb[b_]])
                for b_, (tch, c) in enumerate(gitems()):
                    ls = slice(tch * 512, (tch + 1) * 512)
                    ACT(ggB[c][:, ls], banks[b_][:], AF.Square, [bb[b_]], [Bg[c]], scale=0.21145921366322695)

            def B1b(it):
                for b_, (tch, c) in enumerate(gitems()):
                    ls = slice(tch * 512, (tch + 1) * 512)
                    STT(dve, ggB[c][:, ls], ggB[c][:, ls], 1.0, banks[b_][:], ALU.add, ALU.mult, [Bg[c], bb[b_]], [Bg[c]])
                for b_, (tch, c) in enumerate(gitems()):
                    ls = slice(tch * 512, (tch + 1) * 512)
                    ACT(ggB[c][:, ls], ggB[c][:, ls], AF.Sigmoid, [Bg[c]], [Bg[c]], scale=1.5957691216057308)
                for b_, (tch, c) in enumerate(gitems()):
                    ls = slice(tch * 512, (tch + 1) * 512)
                    TT(dve, ggB[c][:, ls], ggB[c][:, ls], banks[b_][:], ALU.mult, [Bg[c], bb[b_]], [Bg[c]])

            NUG = KR // 2

            def B2(it):
                half, ug = it
                yo = 2 * (ug % 2)
                for tch in range(NTH):
                    for c in range(2):
                        ls = slice(tch * 512, (tch + 1) * 512)
                        TT(dve, yT[:, yo + c, ls], ggB[c][:, ls], hsB[c][:, ls], ALU.mult, [Bg[c], Bh[c]], [B_["yT"]])

            def B3(it):
                half, ug = it
                if ug % 2 == 0 and ug + 1 < NUG:
                    return
                u0 = ug - 1 if ug % 2 == 1 else ug
                nk = 4 if ug % 2 == 1 else 2
                ko = 0 if ug % 2 == 1 else 2 * (ug % 2)
                c0 = u0 * 256
                pieces = [((lambda T, j=j, c0=c0, nk=nk: T["lru_w_out"][j][c0:c0 + nk * 128, :].rearrange("(k p) c -> p k c", p=128)), 0, nk, D)]
                wot, wob = w_get(pieces)
                wo = wview(wot, 0, nk, D)
                for o in range(KD):
                    for tch in range(NTH):
                        ls = slice(tch * 512, (tch + 1) * 512)
                        bi = 6 + (rot["ev"] % 2)
                        rot["ev"] += 1
                        for k in range(nk):
                            MM(banks[bi][:], wo[:, k, o * 128:(o + 1) * 128], yT[:, ko + k, ls], k == 0, k == nk - 1,
                               [wob, B_["yT"]], [bb[bi]])
                        resid_update(bi, o, half * NTH + tch, sub)

            n_it = len(its)
            first_chunk = A1(its[0], defer=True)

            def first_xb(tcn):
                if tcn < NTH:
                    first_chunk(tcn)
            sub_norm(sub, after_apply=first_xb)
            A2a(its[0])
            A2b(its[0])
            A2c(its[0])
            A3(its[0])
            if n_it > 1:
                A1(its[1])
            for i in range(n_it):
                B1a(its[i])
                if i + 1 < n_it:
                    A2a(its[i + 1])
                B1b(its[i])
                if i + 1 < n_it:
                    A2b(its[i + 1])
                B2(its[i])
                B3(its[i])
                if i + 2 < n_it:
                    A1(its[i + 2])
                if i + 1 < n_it:
                    A2c(its[i + 1])
                    A3(its[i + 1])
            phase_barrier()

    def gelu_tanh(gg, ggb, cs, bank, bankb):
        ACT(gg[:, cs], bank[:], AF.Square, [bankb], [ggb])
        TS(dve, gg[:, cs], gg[:, cs], 0.044715, 1.0, ALU.mult, ALU.add, [ggb], [ggb])
        TT(dve, gg[:, cs], gg[:, cs], bank[:], ALU.mult, [ggb, bankb], [ggb])
        ACT(gg[:, cs], gg[:, cs], AF.Sigmoid, [ggb], [ggb], scale=1.5957691216057308)
        TT(dve, gg[:, cs], gg[:, cs], bank[:], ALU.mult, [ggb, bankb], [ggb])

    def epilogue():
        odst = out_ap.rearrange("(k p) s -> p k s", p=128)
        with tl("onT0", [128, KD, 512], F32) as onT0, tl("onT1", [128, KD, 512], F32) as onT1:
            onTs = [onT0, onT1]
            onb = [Buf("onT0"), Buf("onT1")]
            osem = [DmaSem(P, "osem0"), DmaSem(P, "osem1")]
            last = {}
            if final:
                TS(dve, fgs[:, :], sm[:, cfg.o_fg:cfg.o_fg + KD], float(np.sqrt(D)), None, ALU.mult, None, [smb], [fgb])
                iv = norm_stats(0)
            for tcn in range(NT):
                cs = slice(tcn * 512, (tcn + 1) * 512)
                if final:
                    iv_next = norm_stats(tcn + 1) if tcn + 1 < NT else None
                    o_ = onTs[tcn % 2]
                    norm_apply(tcn, iv, fgs, None, lambda k, t_, o_=o_: o_[:, k, :], [onb[tcn % 2]] * NT, [fgb])
                    iv = iv_next
                    tk = OP(sp, (lambda h, o_=o_, cs=cs: [h.dma_start(out=odst[:, :, cs], in_=o_[:, :, :])]),
                            reads=[onb[tcn % 2]], dma_sem=osem[tcn % 2], ndma=1)
                else:
                    tk = OP(sp, (lambda h, cs=cs: [h.dma_start(out=odst[:, :, cs], in_=xT[:, :, cs])]),
                            reads=[xb[tcn]], dma_sem=osem[tcn % 2], ndma=1)
                last[tcn % 2] = tk
            if not planning:
                for tk in last.values():
                    P.wait_ticket(sp, tk)
                for tk in last.values():
                    P.wait_ticket(act, tk)

    prologue()
    layers_in = []
    for (l, s_) in stages:
        if l not in layers_in:
            layers_in.append(l)
    if layers_in:
        mod_enqueue(layers_in[0])
    for (l, s_) in stages:
        cur["l"] = l
        nxt = [m for m in layers_in if m > l]
        cur["next"] = nxt[0] if nxt else None
        if s_ == 0:
            ffn_phase(l, 0, 0)
        elif s_ == 2:
            ffn_phase(l, 1, 2)
        else:
            if cur["next"] is not None:
                mod_enqueue(cur["next"])
            if l % 2 == 0:
                attn_phase(l, l // 2)
            else:
                lru_phase(l, l // 2)
    epilogue()
    if planning:
        return units
    assert wstate["consumed"] == len(units), (wstate, len(units))
    P.emit()
    return nc, P


def build_program(cfg, stages=None, final=True):
    plan = build(cfg, stages, final, plan=None)
    return build(cfg, stages, final, plan=plan)


_W_NAMES = ["mod_w", "ffn_w_gu", "ffn_w_down", "sb_w_qkv", "sb_w_o", "lru_w_in", "lru_w_r", "lru_w_i", "lru_w_out"]


def make_in_maps(cfg, inputs, B):
    consts = host_consts()
    shared = {n: np.ascontiguousarray(np.asarray(inputs[n], np.float32)) for n in _W_NAMES}
    in_maps = []
    for b in range(B):
        m = {"x": np.ascontiguousarray(np.asarray(inputs["x"][b], np.float32).T),
             "smalls": host_smalls(cfg, b, inputs), "consts": consts}
        m.update(shared)
        in_maps.append(m)
    return in_maps


_CACHE = {}


def kernel(**inputs):
    x = np.asarray(inputs["x"])
    B, S, D = x.shape
    cfg = Cfg(S=S, D=D)
    key = (S, D)
    if key not in _CACHE:
        _CACHE[key] = build_program(cfg)
    nc, _ = _CACHE[key]
    in_maps = make_in_maps(cfg, inputs, B)
    res = run_bass_kernel_spmd(nc, in_maps, core_ids=list(range(B)))
    out = np.stack([np.ascontiguousarray(np.asarray(r["out"], np.float32).T) for r in res.results], axis=0)
    return out
```

```python
import numpy as np
import concourse.bass as bass
import concourse.mybir as mybir
from concourse.bass_utils import run_bass_kernel_spmd

F32 = mybir.dt.float32
BF16 = mybir.dt.bfloat16
AF = mybir.ActivationFunctionType
ALU = mybir.AluOpType

EPS = 1e-6
LRU_C = 8.0
SERIALIZE = False


class Buf:
    __slots__ = ("name", "writer", "readers")

    def __init__(self, name):
        self.name = name
        self.writer = None
        self.readers = {}


class _Rec:
    __slots__ = ("kind", "fn", "inc", "sem", "val")

    def __init__(self, kind, fn=None, sem=None, val=None):
        self.kind = kind
        self.fn = fn
        self.inc = False
        self.sem = sem
        self.val = val


class Eng:
    def __init__(self, name, sem, raw_safe):
        self.name = name
        self.sem = sem
        self.recs = []
        self.count = 0
        self.last_op = None
        self.pending = []
        self.waited = {}
        self.raw_safe = raw_safe


class Ticket:
    __slots__ = ("eng", "rec", "val", "sem")

    def __init__(self, eng=None, rec=None, sem=None, val=None):
        self.eng = eng
        self.rec = rec
        self.sem = sem
        self.val = val


class DmaSem:
    def __init__(self, prog, name):
        self.sem = prog.nc.alloc_semaphore(name=name)
        self.count = 0


class Prog:
    def __init__(self, nc):
        self.nc = nc
        self.engs = {}
        self.n_wait = 0
        self.n_ops = 0

    def add_engine(self, name, raw_safe=False):
        sem = self.nc.alloc_semaphore(name="sem_" + name)
        e = Eng(name, sem, raw_safe)
        self.engs[name] = e
        return e

    def _resolve(self, t):
        if t.val is not None:
            return t.sem, t.val
        e = t.eng
        rec = t.rec
        if rec.val is None:
            last = e.last_op
            assert not last.inc, "lazy-inc bookkeeping"
            last.inc = True
            e.count += 1
            for r in e.pending:
                r.val = e.count
            e.pending = []
        t.sem, t.val = e.sem, rec.val
        assert t.val is not None
        return t.sem, t.val

    def _wait(self, eng, t):
        sem, val = self._resolve(t)
        key = id(sem)
        if eng.waited.get(key, 0) >= val:
            return
        eng.waited[key] = val
        eng.recs.append(_Rec("wait", sem=sem, val=val))
        self.n_wait += 1

    def op(self, eng, fn, reads=(), writes=(), dma_sem=None, ndma=1, eager=True):
        deps = []
        for b in reads:
            if b.writer is not None:
                deps.append(("raw", b.writer))
        for b in writes:
            for r in b.readers.values():
                deps.append(("war", r))
            if b.writer is not None:
                deps.append(("waw", b.writer))
        for kind, t in deps:
            if t.eng is eng and dma_sem is None:
                if eng.raw_safe:
                    continue
            self._wait(eng, t)
        rec = _Rec("op", fn=fn)
        eng.recs.append(rec)
        self.n_ops += 1
        if dma_sem is not None:
            dma_sem.count += 16 * ndma
            rec.sem = dma_sem.sem
            rec.kind = "dma"
            tk = Ticket(sem=dma_sem.sem, val=dma_sem.count)
            key = id(dma_sem)
        else:
            eng.pending.append(rec)
            eng.last_op = rec
            tk = Ticket(eng=eng, rec=rec)
            key = id(eng)
            if eager:
                rec.inc = True
                eng.count += 1
                for r in eng.pending:
                    r.val = eng.count
                eng.pending = []
        for b in reads:
            b.readers[key] = tk
        for b in writes:
            b.writer = tk
            b.readers = {}
        if SERIALIZE:
            for e2 in self.engs.values():
                if e2 is not eng or dma_sem is not None:
                    self._wait(e2, tk)
        return tk

    def wait_ticket(self, eng, t):
        self._wait(eng, t)

    def barrier(self, engs):
        tks = []
        for e in engs:
            if e.last_op is not None:
                tks.append(Ticket(eng=e, rec=e.last_op))
        for e in engs:
            for t in tks:
                if t.eng is not e:
                    self._wait(e, t)

    def emit(self):
        nc = self.nc
        with nc.Block() as block:
            for name, e in self.engs.items():
                def body(h, e=e):
                    for r in e.recs:
                        if r.kind == "wait":
                            h.wait_ge(r.sem, r.val)
                        elif r.kind == "dma":
                            for ins in r.fn(h):
                                ins.then_inc(r.sem, 16)
                        else:
                            ins = r.fn(h)
                            if r.inc:
                                ins.then_inc(e.sem, 1)
                getattr(block, name)(body)


class Cfg:
    def __init__(self, S=2048, D=1024, H=16, DFF=2816, DRNN=1280, DEPTH=2, GH=8):
        self.S, self.D, self.H, self.DFF, self.DRNN, self.DEPTH = S, D, H, DFF, DRNN, DEPTH
        self.KD = D // 128
        self.TC = 512
        self.NT = S // 512
        self.NB = S // 128
        self.KF = DFF // 128
        self.KR = DRNN // 128
        self.MODC = 9 * D
        self.MW = 512 if (9 * D) % 512 == 0 else 256
        self.GH = GH
        self.NSB = (DEPTH + 1) // 2
        self.NL = DEPTH // 2
        self.SLOT = 4096
        self.NSLOT = 3
        assert S % 512 == 0 and D % 256 == 0 and DFF % 256 == 0 and DRNN % 256 == 0
        assert H * 64 == D
        KD, KR = self.KD, self.KR
        o = 0
        self.o_c = o; o += KD
        self.o_modb = o; o += DEPTH * 9 * KD
        self.o_g = o; o += DEPTH * 3 * KD
        self.o_fg = o; o += KD
        self.o_lru = o
        self.lru_stride = KR * 4 + 4 * KR
        o += max(self.NL, 1) * self.lru_stride
        self.NSM = o
        self.NCONST = 5 * 128


def host_consts():
    j = np.arange(128)[:, None]
    s = np.arange(128)[None, :]
    ident = (j == s).astype(np.float32)
    ones = np.ones((128, 128), np.float32)
    negUI = -(j >= s).astype(np.float32)
    negLs = -(j < s).astype(np.float32)
    tri = (j < s).astype(np.float32)
    return np.ascontiguousarray(np.concatenate([ident, ones, negUI, negLs, tri], axis=1))


def colmajor(v):
    v = np.asarray(v, np.float32).reshape(-1, 128)
    return np.ascontiguousarray(v.T)


def host_smalls(cfg, b, inputs):
    sm = np.zeros((128, cfg.NSM), np.float32)
    KD, KR = cfg.KD, cfg.KR
    sm[:, cfg.o_c:cfg.o_c + KD] = colmajor(inputs["c"][b])
    for l in range(cfg.DEPTH):
        sm[:, cfg.o_modb + l * 9 * KD: cfg.o_modb + (l + 1) * 9 * KD] = colmajor(inputs["mod_b"][l])
        sm[:, cfg.o_g + l * 3 * KD: cfg.o_g + (l + 1) * 3 * KD] = colmajor(inputs["norm_g"][l].reshape(-1))
    sm[:, cfg.o_fg:cfg.o_fg + KD] = colmajor(inputs["final_norm_g"])
    for jl in range(cfg.NL):
        o = cfg.o_lru + jl * cfg.lru_stride
        cw = np.asarray(inputs["lru_conv_w"][jl], np.float32)
        sm[:, o:o + KR * 4] = cw.T.reshape(KR, 128, 4).transpose(1, 0, 2).reshape(128, KR * 4)
        o += KR * 4
        for nm in ("lru_conv_b", "lru_b_r", "lru_b_i", "lru_lambda"):
            sm[:, o:o + KR] = colmajor(inputs[nm][jl])
            o += KR
    return sm


def build(cfg, stages=None, final=True, plan=None):
    planning = plan is None
    S, D, KD, NT, NB, KF, KR, DFF, DRNN = cfg.S, cfg.D, cfg.KD, cfg.NT, cfg.NB, cfg.KF, cfg.KR, cfg.DFF, cfg.DRNN
    DEPTH = cfg.DEPTH
    if stages is None:
        stages = [(l, s) for l in range(DEPTH) for s in range(3)]
    nc = bass.Bass("TRN2", target_bir_lowering=False)
    T = {}

    def din(name, shape):
        T[name] = nc.dram_tensor(name, list(shape), F32, kind="ExternalInput").ap()

    din("x", [D, S])
    din("smalls", [128, cfg.NSM])
    din("consts", [128, cfg.NCONST])
    din("mod_w", [DEPTH, D, 9 * D])
    din("ffn_w_gu", [DEPTH, 2, D, 2 * DFF])
    din("ffn_w_down", [DEPTH, 2, DFF, D])
    din("sb_w_qkv", [cfg.NSB, D, 3 * D])
    din("sb_w_o", [cfg.NSB, D, D])
    din("lru_w_in", [max(cfg.NL, 1), D, 2 * DRNN])
    din("lru_w_r", [max(cfg.NL, 1), KR, 128, 128])
    din("lru_w_i", [max(cfg.NL, 1), KR, 128, 128])
    din("lru_w_out", [max(cfg.NL, 1), DRNN, D])
    out_ap = nc.dram_tensor("out", [D, S], F32, kind="ExternalOutput").ap()

    P = Prog(nc)
    pe = P.add_engine("tensor", raw_safe=True)
    act = P.add_engine("scalar")
    dve = P.add_engine("vector")
    pool = P.add_engine("gpsimd")
    sp = P.add_engine("sync")
    compute_engs = [pe, act, dve, pool]

    def OP(eng, fn, reads=(), writes=(), **kw):
        if planning:
            return None
        return P.op(eng, fn, reads, writes, **kw)

    def sb(name, shape, dt):
        return nc.alloc_sbuf_tensor(name, list(shape), dt)

    xT = sb("xT", [128, KD, S], F32)
    hT = sb("hT", [128, KD, S], BF16)
    xb = [Buf(f"x{t}") for t in range(NT)]
    hb = [Buf(f"h{t}") for t in range(NT)]
    sm = sb("sm", [128, cfg.NSM], F32)
    smb = Buf("sm")
    ident = sb("ident", [128, 128], F32)
    cbf = sb("cbf", [128, 3 * 128], BF16)
    tri = sb("tri", [128, 128], F32)
    cb = Buf("consts")
    ones_bf = cbf[:, 0:128]
    negUI = cbf[:, 128:256]
    negLs = cbf[:, 256:384]
    zeros_bf = sb("zeros", [128, 128], BF16)
    modv = [sb(f"modv{i}", [128, 9 * KD], F32) for i in range(2)]
    Asc = [sb(f"Asc{i}", [128, 3 * KD], F32) for i in range(2)]
    gsc = [sb(f"gsc{i}", [128, 3 * KD], F32) for i in range(2)]
    modb = [Buf("modv0"), Buf("modv1")]
    fgb = Buf("fgs")
    cact = sb("cact", [128, KD], BF16)
    cactb = Buf("cact")
    fgs = sb("fgs", [128, KD], F32)
    lruc = sb("lruc", [128, KR], F32)
    lrub = Buf("lruc")
    sqk = [sb(f"sqk{i}", [128, 512], BF16) for i in range(4)]
    sqkb = [Buf(f"sqk{i}") for i in range(4)]
    invs = [sb(f"inv{i}", [128, 512], F32) for i in range(2)]
    invsb = [Buf(f"inv{i}") for i in range(2)]
    ntmp = [sb(f"ntmp{i}", [128, 512], F32) for i in range(3)]
    ntmpb = [Buf(f"ntmp{i}") for i in range(3)]
    xstageb = [Buf(f"xstage{i}") for i in range(2)]
    c_sem = DmaSem(P, "csem")
    wslots = [sb(f"wslot{i}", [128, cfg.SLOT], BF16) for i in range(cfg.NSLOT)]
    wbufs = [Buf(f"wslot{i}") for i in range(cfg.NSLOT)]
    wsems = [DmaSem(P, f"wsem{i}") for i in range(cfg.NSLOT)]
    banks = [nc.alloc_psum_tensor(f"bank{i}", [128, 512], F32) for i in range(8)]
    bb = [Buf(f"bank{i}") for i in range(8)]

    units = [] if planning else plan
    wstate = {"consumed": 0, "issued": 0}

    def w_issue(i):
        slot = i % cfg.NSLOT
        tile = wslots[slot]
        pieces = units[i]

        def fn(h, tile=tile, pieces=pieces):
            res = []
            for (src, off, a, b) in pieces:
                dst = tile[:, off:off + a * b].rearrange("p (a b) -> p a b", a=a)
                res.append(h.dma_start(out=dst, in_=src(T)))
            return res
        P.op(pool, fn, writes=[wbufs[slot]], dma_sem=wsems[slot], ndma=len(pieces))

    def w_get(pieces):
        i = wstate["consumed"]
        wstate["consumed"] += 1
        if planning:
            units.append(pieces)
            return wslots[0], wbufs[0]
        while wstate["issued"] < min(len(units), i + cfg.NSLOT):
            w_issue(wstate["issued"])
            wstate["issued"] += 1
        slot = i % cfg.NSLOT
        return wslots[slot], wbufs[slot]

    def wview(tile, off, a, b):
        return tile[:, off:off + a * b].rearrange("p (a b) -> p a b", a=a)

    def MM(out, lhsT, rhs, start, stop, reads, writes, skip=False):
        if skip:
            OP(pe, lambda h: h.matmul(out, lhsT=lhsT, rhs=rhs, start=start, stop=stop, skip_group_check=True),
               reads, writes, eager=bool(stop))
        else:
            OP(pe, lambda h: h.matmul(out, lhsT=lhsT, rhs=rhs, start=start, stop=stop), reads, writes,
               eager=bool(stop))

    def ACT(out, in_, func, reads, writes, bias=None, scale=None):
        kw = {}
        if bias is not None:
            kw["bias"] = bias
        if scale is not None:
            kw["scale"] = scale
        return OP(act, lambda h: h.activation(out=out, in_=in_, func=func, **kw), reads, writes)

    def TT(eng, out, in0, in1, op, reads, writes):
        return OP(eng, lambda h: h.tensor_tensor(out=out, in0=in0, in1=in1, op=op), reads, writes)

    def TS(eng, out, in0, s1, s2, op0, op1, reads, writes):
        if op1 is None:
            return OP(eng, lambda h: h.tensor_scalar(out=out, in0=in0, scalar1=s1, scalar2=None, op0=op0),
                      reads, writes)
        return OP(eng, lambda h: h.tensor_scalar(out=out, in0=in0, scalar1=s1, scalar2=s2, op0=op0, op1=op1),
                  reads, writes)

    def STT(eng, out, in0, scalar, in1, op0, op1, reads, writes):
        return OP(eng, lambda h: h.scalar_tensor_tensor(out=out, in0=in0, scalar=scalar, in1=in1, op0=op0, op1=op1),
                  reads, writes)

    def CP(eng, out, in_, reads, writes):
        return OP(eng, lambda h: h.tensor_copy(out=out, in_=in_), reads, writes)

    def phase_barrier():
        if not planning:
            P.barrier(compute_engs)

    rot = {"gu": 0, "dn": 0, "nt": 0, "ev": 0, "nm": 0, "sq": 0, "nv": 0}

    def tl(name, shape, dt):
        rot["nm"] += 1
        return nc.sbuf_tensor(f"{name}_{rot['nm']}", list(shape), dt)

    def prologue():
        OP(sp, lambda h: [h.dma_start(out=sm[:], in_=T["smalls"][:, :]),
                          h.dma_start(out=ident[:], in_=T["consts"][:, 0:128]),
                          h.dma_start(out=tri[:], in_=T["consts"][:, 512:640])],
           writes=[smb, cb], dma_sem=c_sem, ndma=3)
        c2 = DmaSem(P, "csem2")
        OP(pool, lambda h: [h.dma_start(out=cbf[:], in_=T["consts"][:, 128:512])],
           writes=[cb], dma_sem=c2, ndma=1)
        OP(dve, lambda h: h.memset(zeros_bf[:], 0.0), writes=[cb])
        ACT(cact[:], sm[:, cfg.o_c:cfg.o_c + KD], AF.Silu, [smb], [cactb])
        x_load_chunks(range(0, 1))
        if stages:
            cur["l"] = stages[0][0]
            mod_require(stages[0][0], stages[0][1], 7)
            gate = wbufs[(wstate["consumed"] - 1) % cfg.NSLOT].writer if not planning else None
            if gate is not None:
                P.wait_ticket(sp, gate)
        x_load_chunks(range(1, NT))

    xload = {"done": NB, "last": None}
    xsems = [DmaSem(P, f"xs{i}") for i in range(NT)]

    def x_load_chunks(tcns):
        xsrc = T["x"].rearrange("(k p) s -> p k s", p=128)
        for tcn in tcns:
            cs = slice(tcn * 512, (tcn + 1) * 512)
            OP(sp, (lambda h, cs=cs: [h.dma_start(out=xT[:, :, cs], in_=xsrc[:, :, cs])]),
               writes=[xb[tcn]], dma_sem=xsems[tcn], ndma=1)

    mod_pending = []
    mod_done = set()
    mod_fin = set()
    MWc = cfg.MW
    nfl_ = MWc // 128
    NMU = cfg.MODC // MWc

    def mod_enqueue(l):
        for u in range(NMU):
            if (l, u) not in mod_done and (l, u) not in mod_pending:
                mod_pending.append((l, u))

    def mod_unit(l, u, bank_i):
        mod_done.add((l, u))
        pieces = [((lambda T, l=l, u=u: T["mod_w"][l].rearrange("(k p) c -> p k c", p=128)[:, :, u * MWc:(u + 1) * MWc]),
                   0, KD, MWc)]
        wt, wb = w_get(pieces)
        wv = wview(wt, 0, KD, MWc)
        mbank = banks[bank_i]
        for fl in range(nfl_):
            for k in range(KD):
                MM(mbank[:, fl:fl + 1], wv[:, k, fl * 128:(fl + 1) * 128], cact[:, k:k + 1],
                   k == 0, k == KD - 1, [wb, cactb], [bb[bank_i]])
        ob = cfg.o_modb + l * 9 * KD + u * nfl_
        TT(dve, modv[l % 2][:, u * nfl_:(u + 1) * nfl_], mbank[:, 0:nfl_], sm[:, ob:ob + nfl_], ALU.add,
           [bb[bank_i], smb], [modb[l % 2]])

    def mod_pump(n, bank_i):
        for _ in range(n):
            if not mod_pending:
                return
            l, u = mod_pending.pop(0)
            mod_unit(l, u, bank_i)

    def mod_require(l, sub, bank_i):
        u_lo = (sub * 3 * D) // MWc
        u_hi = ((sub + 1) * 3 * D + MWc - 1) // MWc
        for u in range(u_lo, u_hi):
            if (l, u) not in mod_done:
                if (l, u) in mod_pending:
                    mod_pending.remove((l, u))
                mod_unit(l, u, bank_i)
        if (l, sub) in mod_fin:
            return
        mod_fin.add((l, sub))
        mv, mb_ = modv[l % 2], modb[l % 2]
        og = cfg.o_g + l * 3 * KD
        sc = mv[:, (sub * 3 + 1) * KD:(sub * 3 + 2) * KD]
        gt = mv[:, (sub * 3 + 2) * KD:(sub * 3 + 3) * KD]
        A = Asc[l % 2][:, sub * KD:(sub + 1) * KD]
        STT(dve, A, sc, 1.0, sm[:, og + sub * KD: og + (sub + 1) * KD], ALU.add, ALU.mult, [mb_, smb], [mb_])
        TS(dve, A, A, float(np.sqrt(D)), None, ALU.mult, None, [mb_], [mb_])
        mw = 1.0 if sub == 1 else 0.5
        TS(dve, gsc[l % 2][:, sub * KD:(sub + 1) * KD], gt, 1.0, mw, ALU.add, ALU.mult, [mb_], [mb_])

    def norm_stats(tcn):
        cs = slice(tcn * 512, (tcn + 1) * 512)
        for k in range(KD):
            r = rot["sq"] % 4
            rot["sq"] += 1
            ACT(sqk[r][:], xT[:, k, cs], AF.Square, [xb[tcn]], [sqkb[r]])
            MM(banks[7][:], ones_bf, sqk[r][:], k == 0, k == KD - 1, [sqkb[r], cb], [bb[7]])
        iv = rot["nv"] % 2
        rot["nv"] += 1
        ACT(invs[iv][:], banks[7][:], AF.Ln, [bb[7]], [invsb[iv]], bias=float(D * EPS), scale=1.0)
        ACT(invs[iv][:], invs[iv][:], AF.Exp, [invsb[iv]], [invsb[iv]], scale=-0.5)
        return iv

    def norm_apply(tcn, iv, Acols, shiftcols, dst_fn, dstbufs, areads):
        cs = slice(tcn * 512, (tcn + 1) * 512)
        if tcn == NT - 1 and xload["last"] is not None:
            if not planning:
                P.wait_ticket(act, xload["last"])
                P.wait_ticket(dve, xload["last"])
            xload["last"] = None
        for k in range(KD):
            if shiftcols is None:
                STT(dve, dst_fn(k, tcn), xT[:, k, cs], Acols[:, k:k + 1], invs[iv][:], ALU.mult, ALU.mult,
                    [xb[tcn], invsb[iv]] + areads, [dstbufs[tcn]])
                continue
            i = rot["nt"] % 3
            rot["nt"] += 1
            STT(dve, ntmp[i][:], xT[:, k, cs], Acols[:, k:k + 1], invs[iv][:], ALU.mult, ALU.mult,
                [xb[tcn], invsb[iv]] + areads, [ntmpb[i]])
            if k % 2 == 0:
                ACT(dst_fn(k, tcn), ntmp[i][:], AF.Identity, [ntmpb[i]] + areads, [dstbufs[tcn]],
                    bias=shiftcols[:, k:k + 1], scale=1.0)
            else:
                TS(dve, dst_fn(k, tcn), ntmp[i][:], shiftcols[:, k:k + 1], None, ALU.add, None,
                   [ntmpb[i]] + areads, [dstbufs[tcn]])

    def norm_phase(Acols, shiftcols, dst_fn, dstbufs, areads, after_apply=None):
        iv = norm_stats(0)
        for tcn in range(NT):
            iv_next = norm_stats(tcn + 1) if tcn + 1 < NT else None
            norm_apply(tcn, iv, Acols, shiftcols, dst_fn, dstbufs, areads)
            if after_apply is not None and tcn >= 1:
                after_apply(tcn - 1)
            iv = iv_next
        if after_apply is not None:
            after_apply(NT - 1)

    cur = {"l": 0}

    def sub_norm(sub, after_apply=None):
        l = cur["l"]
        mod_require(l, sub, 7)
        norm_phase(Asc[l % 2][:, sub * KD:(sub + 1) * KD], modv[l % 2][:, (sub * 3) * KD:(sub * 3 + 1) * KD],
                   lambda k, tcn: hT[:, k, tcn * 512:(tcn + 1) * 512], hb, [modb[l % 2]], after_apply=after_apply)

    def resid_update(bank_i, o, tcn, sub):
        l = cur["l"]
        cs = slice(tcn * 512, (tcn + 1) * 512)
        STT(dve, xT[:, o, cs], banks[bank_i][:], gsc[l % 2][:, sub * KD + o: sub * KD + o + 1], xT[:, o, cs],
            ALU.mult, ALU.add, [bb[bank_i], xb[tcn], modb[l % 2]], [xb[tcn]])

    def ffn_phase(l, f, sub):
        mod_require(l, sub, 7)
        GH = cfg.GH
        groups = []
        g0 = 0
        while g0 < KF:
            gn = min(GH, KF - g0)
            groups.append((g0, gn))
            g0 += gn
        with tl("actT", [128, GH, S], BF16) as actT, \
                tl("sg0", [128, 512], F32) as sg0, tl("sg1", [128, 512], F32) as sg1:
            sgs = [sg0, sg1]
            sgb = [Buf("sg0"), Buf("sg1")]
            actb = [Buf(f"act{t}") for t in range(NT)]

            def gu_get(g0, pu):
                c0 = (g0 + 2 * pu) * 128
                pieces = [
                    ((lambda T, l=l, f=f, c0=c0: T["ffn_w_gu"][l, f].rearrange("(k p) c -> p k c", p=128)[:, :, c0:c0 + 256]),
                     0, KD, 256),
                    ((lambda T, l=l, f=f, c0=c0: T["ffn_w_gu"][l, f].rearrange("(k p) c -> p k c", p=128)[:, :, DFF + c0:DFF + c0 + 256]),
                     KD * 256, KD, 256)]
                wt, wb = w_get(pieces)
                return wview(wt, 0, KD, 256), wview(wt, KD * 256, KD, 256), wb

            def gu_emit(wg, wu, wb, pu, cc, tcn):
                fa = 2 * pu + cc
                cs = slice(tcn * 512, (tcn + 1) * 512)
                r = rot["gu"] % 2
                rot["gu"] += 1
                gi, ui = r, 2 + r
                for k in range(KD):
                    MM(banks[gi][:], wg[:, k, cc * 128:(cc + 1) * 128], hT[:, k, cs], k == 0, k == KD - 1,
                       [wb, hb[tcn]], [bb[gi]])
                for k in range(KD):
                    MM(banks[ui][:], wu[:, k, cc * 128:(cc + 1) * 128], hT[:, k, cs], k == 0, k == KD - 1,
                       [wb, hb[tcn]], [bb[ui]])
                ACT(sgs[r][:], banks[gi][:], AF.Silu, [bb[gi]], [sgb[r]])
                TT(dve, actT[:, fa, cs], banks[ui][:], sgs[r][:], ALU.mult, [bb[ui], sgb[r]], [actb[tcn]])

            wg0, wu0, wb0 = gu_get(groups[0][0], 0)

            def first_unit(tcn):
                for cc in range(2):
                    gu_emit(wg0, wu0, wb0, 0, cc, tcn)
            sub_norm(sub, after_apply=first_unit)
            mod_pump(1, 7)
            for gidx, (g0, gn) in enumerate(groups):
                assert gn % 2 == 0
                for pu in range(gn // 2):
                    if gidx == 0 and pu == 0:
                        continue
                    wg, wu, wb = gu_get(g0, pu)
                    for cc in range(2):
                        for tcn in range(NT):
                            gu_emit(wg, wu, wb, pu, cc, tcn)
                    mod_pump(1, 7)
                for du in range(D // 256):
                    pieces = [((lambda T, l=l, f=f, g0=g0, gn=gn, du=du:
                                T["ffn_w_down"][l, f][g0 * 128:(g0 + gn) * 128, :].rearrange("(k p) c -> p k c", p=128)[:, :, du * 256:(du + 1) * 256]),
                               0, gn, 256)]
                    wt, wb = w_get(pieces)
                    wd = wview(wt, 0, gn, 256)
                    for oc in range(2):
                        o = du * 2 + oc
                        for tcn in range(NT):
                            cs = slice(tcn * 512, (tcn + 1) * 512)
                            bi = 4 + (rot["dn"] % 2)
                            rot["dn"] += 1
                            for k in range(gn):
                                MM(banks[bi][:], wd[:, k, oc * 128:(oc + 1) * 128], actT[:, k, cs], k == 0, k == gn - 1,
                                   [wb, actb[tcn]], [bb[bi]])
                            resid_update(bi, o, tcn, sub)
                    mod_pump(1, 7)
            phase_barrier()

    def attn_phase(l, j):
        sub = 1
        mod_require(cur["l"], sub, 7)
        with tl("qT", [128, 2, S], BF16) as qT, tl("kT", [128, 2, S], BF16) as kT, \
                tl("vv", [128, NB, 256], BF16) as vv, tl("oT", [128, 2, S], BF16) as oT, \
                tl("e_sb", [128, 4, 512], F32) as e_sb, tl("sp_sb", [128, 4, 512], BF16) as sp_sb, \
                tl("E_sb", [128, 4, 512], F32) as E_sb, tl("w_sb", [128, 4, 512], BF16) as w_sb:
            qb_, kb_, vb_, ob_ = Buf("qT"), Buf("kT"), Buf("vv"), Buf("oT")
            eb = [Buf(f"e{i}") for i in range(4)]
            spb = [Buf(f"sp{i}") for i in range(4)]
            Eb = [Buf(f"E{i}") for i in range(4)]
            wb_ = [Buf(f"w{i}") for i in range(4)]
            pj = {"i": 0}

            def pbank():
                pj["i"] += 1
                return pj["i"] % 2
            def qk_get(hg):
                c0 = hg * 256
                pieces = [
                    ((lambda T, j=j, c0=c0: T["sb_w_qkv"][j].rearrange("(k p) c -> p k c", p=128)[:, :, c0:c0 + 256]), 0, KD, 256),
                    ((lambda T, j=j, c0=c0: T["sb_w_qkv"][j].rearrange("(k p) c -> p k c", p=128)[:, :, D + c0:D + c0 + 256]), KD * 256, KD, 256)]
                wt, wb = w_get(pieces)
                return wview(wt, 0, KD, 256), wview(wt, KD * 256, KD, 256), wb

            def qk_emit(wq, wk, wb, hp2, tcn):
                cs = slice(tcn * 512, (tcn + 1) * 512)
                bi = pbank()
                for k in range(KD):
                    MM(banks[bi][:], wq[:, k, hp2 * 128:(hp2 + 1) * 128], hT[:, k, cs], k == 0, k == KD - 1,
                       [wb, hb[tcn]], [bb[bi]])
                ACT(qT[:, hp2, cs], banks[bi][:], AF.Identity, [bb[bi]], [qb_], bias=0.0, scale=0.125)
                bi = pbank()
                for k in range(KD):
                    MM(banks[bi][:], wk[:, k, hp2 * 128:(hp2 + 1) * 128], hT[:, k, cs], k == 0, k == KD - 1,
                       [wb, hb[tcn]], [bb[bi]])
                CP(dve, kT[:, hp2, cs], banks[bi][:], [bb[bi]], [kb_])

            wq0, wk0, wb0 = qk_get(0)

            def first_qk(tcn):
                for hp2 in range(2):
                    qk_emit(wq0, wk0, wb0, hp2, tcn)
            sub_norm(sub, after_apply=first_qk)
            for hg in range(D // 256):
                c0 = hg * 256
                if hg > 0:
                    wq, wk, wb = qk_get(hg)
                    for hp2 in range(2):
                        for tcn in range(NT):
                            qk_emit(wq, wk, wb, hp2, tcn)
                pieces = [((lambda T, j=j, c0=c0: T["sb_w_qkv"][j].rearrange("(k p) c -> p k c", p=128)[:, :, 2 * D + c0:2 * D + c0 + 256]), 0, KD, 256)]
                wt, wb = w_get(pieces)
                wvv = wview(wt, 0, KD, 256)
                for tb in range(NB):
                    bi = pbank()
                    for k in range(KD):
                        MM(banks[bi][:, 0:256], hT[:, k, tb * 128:(tb + 1) * 128], wvv[:, k, :], k == 0, k == KD - 1,
                           [wb, hb[tb // 4]], [bb[bi]])
                    if tb % 2:
                        CP(dve, vv[:, tb, :], banks[bi][:, 0:256], [bb[bi]], [vb_])
                    else:
                        ACT(vv[:, tb, :], banks[bi][:, 0:256], AF.Copy, [bb[bi]], [vb_])
                pieces = [((lambda T, j=j, c0=c0: T["sb_w_o"][j][c0:c0 + 256, :].rearrange("(k p) c -> p k c", p=128)), 0, 2, D)]
                wot, wob = w_get(pieces)
                wo = wview(wot, 0, 2, D)
                prs = [slice(0, 64), slice(64, 128)]
                for qc in range(NT):
                    q0 = qc * 512
                    for bi in (2, 3, 4, 5, 6, 7):
                        MM(banks[bi][:], zeros_bf[:, :], hT[:, 0, 0:512], True, True, [cb, hb[0]], [bb[bi]])
                    kbs = list(range(4 * qc + 3, -1, -1))

                    def geom(kb, qc=qc):
                        jd = kb - 4 * qc
                        cst = 128 * jd if jd > 0 else 0
                        return jd, cst

                    def zmm(s_, kb, q0=q0):
                        jd, cst = geom(kb)
                        hp2, pr = s_ // 2, prs[s_ % 2]
                        zb = s_ % 2
                        MM(banks[zb][:, cst:512], kT[pr, hp2, kb * 128:(kb + 1) * 128], qT[pr, hp2, q0 + cst:q0 + 512],
                           True, True, [kb_, qb_], [bb[zb]])

                    def e_op(s_, kb):
                        jd, cst = geom(kb)
                        zb = s_ % 2
                        ACT(e_sb[:, s_, cst:512], banks[zb][:, cst:512], AF.Exp, [bb[zb]], [eb[s_]])
                        if jd >= 0:
                            TT(dve, e_sb[:, s_, cst:cst + 128], e_sb[:, s_, cst:cst + 128], tri[:, :], ALU.mult,
                               [eb[s_], cb], [eb[s_]])

                    def sp_op(s_, kb):
                        jd, cst = geom(kb)
                        ACT(sp_sb[:, s_, cst:512], e_sb[:, s_, cst:512], AF.Ln, [eb[s_]], [spb[s_]], bias=1.0, scale=1.0)

                    def r1_op(s_, kb):
                        jd, cst = geom(kb)
                        MM(banks[2 + s_][:, cst:512], negUI, sp_sb[:, s_, cst:512], False, True, [cb, spb[s_]], [bb[2 + s_]], skip=True)

                    def E_op(s_, kb):
                        jd, cst = geom(kb)
                        ACT(E_sb[:, s_, cst:512], banks[2 + s_][:, cst:512], AF.Exp, [bb[2 + s_]], [Eb[s_]])

                    def w_op(s_, kb):
                        jd, cst = geom(kb)
                        TT(dve, w_sb[:, s_, cst:512], e_sb[:, s_, cst:512], E_sb[:, s_, cst:512], ALU.mult,
                           [eb[s_], Eb[s_]], [wb_[s_]])

                    def o_op(s_, kb):
                        jd, cst = geom(kb)
                        hp2, hd = s_ // 2, s_ % 2
                        MM(banks[6 + hp2][hd * 64:(hd + 1) * 64, cst:512], vv[:, kb, hp2 * 128 + hd * 64: hp2 * 128 + (hd + 1) * 64],
                           w_sb[:, s_, cst:512], False, True, [vb_, wb_[s_]], [bb[6 + hp2]], skip=True)

                    def r2_op(s_, kb):
                        jd, cst = geom(kb)
                        MM(banks[2 + s_][:, cst:512], negLs, sp_sb[:, s_, cst:512], False, True, [cb, spb[s_]], [bb[2 + s_]], skip=True)

                    zmm(0, kbs[0])
                    zmm(1, kbs[0])
                    for ii, kb in enumerate(kbs):
                        last = ii + 1 == len(kbs)
                        e_op(0, kb)
                        e_op(1, kb)
                        zmm(2, kb)
                        zmm(3, kb)
                        sp_op(0, kb)
                        sp_op(1, kb)
                        r1_op(0, kb)
                        r1_op(1, kb)
                        e_op(2, kb)
                        e_op(3, kb)
                        if not last:
                            zmm(0, kbs[ii + 1])
                            zmm(1, kbs[ii + 1])
                        sp_op(2, kb)
                        sp_op(3, kb)
                        r1_op(2, kb)
                        r1_op(3, kb)
                        for s_ in range(4):
                            E_op(s_, kb)
                        for s_ in range(4):
                            w_op(s_, kb)
                        for pair in (0, 2):
                            o_op(pair, kb)
                            o_op(pair + 1, kb)
                            if not last:
                                r2_op(pair, kb)
                                r2_op(pair + 1, kb)
                    ACT(oT[:, 0, q0:q0 + 512], banks[6][:, :], AF.Copy, [bb[6]], [ob_])
                    CP(dve, oT[:, 1, q0:q0 + 512], banks[7][:, :], [bb[7]], [ob_])
                for o in range(KD):
                    for tcn in range(NT):
                        cs = slice(tcn * 512, (tcn + 1) * 512)
                        bi = pbank()
                        for k in range(2):
                            MM(banks[bi][:], wo[:, k, o * 128:(o + 1) * 128], oT[:, k, cs], k == 0, k == 1, [wob, ob_], [bb[bi]])
                        resid_update(bi, o, tcn, sub)
                mod_pump(5, 0)
            phase_barrier()

    def lru_phase(l, j):
        sub = 1
        mod_require(cur["l"], sub, 7)
        ol = cfg.o_lru + j * cfg.lru_stride
        o_cw = ol
        o_cb = ol + KR * 4
        o_br = o_cb + KR
        o_bi = o_br + KR
        o_lam = o_bi + KR
        NH = 2
        TH = S // NH
        NTH = TH // 512
        with tl("xbp0", [128, TH + 4], F32) as xbp0, tl("xbp1", [128, TH + 4], F32) as xbp1, \
                tl("xc0", [128, TH], F32) as xc0, tl("xc1", [128, TH], F32) as xc1, \
                tl("xcb0", [128, TH], BF16) as xcb0, tl("xcb1", [128, TH], BF16) as xcb1, \
                tl("ra0", [128, TH], F32) as ra0, tl("ra1", [128, TH], F32) as ra1, \
                tl("ib0", [128, TH], F32) as ib0, tl("ib1", [128, TH], F32) as ib1, \
                tl("yT", [128, 4, TH], BF16) as yT, tl("wri", [128, 2 * KR * 128], BF16) as wri, \
                tl("lt", [128, KR], F32) as lt, tl("halo", [128, KR, 4], F32) as halo, \
                tl("hlast", [128, KR], F32) as hlast, \
                tl("hsB0", [128, TH], F32) as hsB0, tl("hsB1", [128, TH], F32) as hsB1, \
                tl("ggB0", [128, TH], F32) as ggB0, tl("ggB1", [128, TH], F32) as ggB1:
            hsB = [hsB0, hsB1]
            ggB = [ggB0, ggB1]
            Bh = [Buf("hsB0"), Buf("hsB1")]
            Bg = [Buf("ggB0"), Buf("ggB1")]
            xbp = [xbp0, xbp1]
            xc = [xc0, xc1]
            xcb = [xcb0, xcb1]
            ra = [ra0, ra1]
            ib = [ib0, ib1]
            t2 = [xbp0[:, 4:4 + TH], xbp1[:, 4:4 + TH]]
            B_ = {n: Buf(n) for n in ["yT", "wri", "lt", "halo", "hlast"]}
            Bx = [Buf("xbp0"), Buf("xbp1")]
            Bc = [Buf("xc0"), Buf("xc1")]
            Bcb = [Buf("xcb0"), Buf("xcb1")]
            Br = [Buf("ra0"), Buf("ra1")]
            Bi = [Buf("ib0"), Buf("ib1")]
            cengs = [dve, dve]
            ACT(lt[:], sm[:, o_lam:o_lam + KR], AF.Exp, [smb], [B_["lt"]], scale=-1.0)
            ACT(lt[:], lt[:], AF.Ln, [B_["lt"]], [B_["lt"]], bias=1.0, scale=1.0)
            TS(dve, lruc[:], lt[:], -LRU_C, None, ALU.mult, None, [B_["lt"]], [lrub])
            pieces = [((lambda T, j=j: T["lru_w_r"][j].rearrange("n k c -> k n c")), 0, KR, 128),
                      ((lambda T, j=j: T["lru_w_i"][j].rearrange("n k c -> k n c")), KR * 128, KR, 128)]
            wt, wb = w_get(pieces)
            CP(dve, wri[:, :], wt[:, 0:2 * KR * 128], [wb], [B_["wri"]])
            wr = wri[:, 0:KR * 128].rearrange("p (a b) -> p a b", a=KR)
            wi = wri[:, KR * 128:2 * KR * 128].rearrange("p (a b) -> p a b", a=KR)
            OP(dve, lambda h: h.memset(halo[:], 0.0), writes=[B_["halo"]])
            OP(dve, lambda h: h.memset(hlast[:], 0.0), writes=[B_["hlast"]])
            its = [(half, ug) for half in range(NH) for ug in range(KR // 2)]

            def A1(it, defer=False):
                half, ug = it
                t0 = half * TH
                c0 = ug * 256
                ns = [ug * 2, ug * 2 + 1]
                pieces = [((lambda T, j=j, c0=c0: T["lru_w_in"][j].rearrange("(k p) c -> p k c", p=128)[:, :, DRNN + c0:DRNN + c0 + 256]), 0, KD, 256)]
                wt, wb = w_get(pieces)
                wxb = wview(wt, 0, KD, 256)
                for c in range(2):
                    CP(dve, xbp[c][:, 0:4], halo[:, ns[c], :], [B_["halo"]], [Bx[c]])

                def chunk(tch):
                    for c in range(2):
                        cs = slice(t0 + tch * 512, t0 + (tch + 1) * 512)
                        bi = 6 + (rot["ev"] % 2)
                        rot["ev"] += 1
                        for k in range(KD):
                            MM(banks[bi][:], wxb[:, k, c * 128:(c + 1) * 128], hT[:, k, cs], k == 0, k == KD - 1,
                               [wb, hb[cs.start // 512]], [bb[bi]])
                        ACT(xbp[c][:, 4 + tch * 512: 4 + (tch + 1) * 512], banks[bi][:], AF.Copy, [bb[bi]], [Bx[c]])
                if defer:
                    return chunk
                for tch in range(NTH):
                    chunk(tch)

            def A2a(it):
                half, ug = it
                ns = [ug * 2, ug * 2 + 1]
                for c in range(2):
                    n = ns[c]
                    TS(dve, xc[c][:, :], xbp[c][:, 1:1 + TH], sm[:, o_cw + n * 4: o_cw + n * 4 + 1], sm[:, o_cb + n: o_cb + n + 1],
                       ALU.mult, ALU.add, [Bx[c], smb], [Bc[c]])
                for tap in range(1, 4):
                    for c in range(2):
                        n = ns[c]
                        STT(dve, xc[c][:, :], xbp[c][:, 1 + tap:1 + tap + TH], sm[:, o_cw + n * 4 + tap: o_cw + n * 4 + tap + 1], xc[c][:, :],
                            ALU.mult, ALU.add, [Bx[c], Bc[c], smb], [Bc[c]])
                for c in range(2):
                    ACT(halo[:, ns[c], :], xbp[c][:, TH:TH + 4], AF.Copy, [Bx[c]], [B_["halo"]])
                    ACT(xcb[c][:, :], xc[c][:, :], AF.Copy, [Bc[c]], [Bcb[c]])

            def A2b(it):
                half, ug = it
                ns = [ug * 2, ug * 2 + 1]
                for c in range(2):
                    n = ns[c]
                    for tch in range(NTH):
                        ls = slice(tch * 512, (tch + 1) * 512)
                        bi = 4 + (rot["dn"] % 2)
                        rot["dn"] += 1
                        MM(banks[bi][:], wr[:, n, :], xcb[c][:, ls], True, True, [B_["wri"], Bcb[c]], [bb[bi]])
                        ACT(ra[c][:, ls], banks[bi][:], AF.Sigmoid, [bb[bi], smb], [Br[c]], bias=sm[:, o_br + n:o_br + n + 1], scale=1.0)
                        bi = 4 + (rot["dn"] % 2)
                        rot["dn"] += 1
                        MM(banks[bi][:], wi[:, n, :], xcb[c][:, ls], True, True, [B_["wri"], Bcb[c]], [bb[bi]])
                        ACT(ib[c][:, ls], banks[bi][:], AF.Sigmoid, [bb[bi], smb], [Bi[c]], bias=sm[:, o_bi + n:o_bi + n + 1], scale=1.0)
                for c in range(2):
                    ACT(ra[c][:, :], ra[c][:, :], AF.Exp, [Br[c], lrub], [Br[c]], scale=lruc[:, ns[c]:ns[c] + 1])
                for c in range(2):
                    TT(pool, ib[c][:, :], ib[c][:, :], xc[c][:, :], ALU.mult, [Bi[c], Bc[c]], [Bi[c]])
                    TT(pool, xc[c][:, :], ra[c][:, :], ra[c][:, :], ALU.mult, [Br[c]], [Bc[c]])
                for c in range(2):
                    ACT(xc[c][:, :], xc[c][:, :], AF.Sqrt, [Bc[c]], [Bc[c]], bias=1.0, scale=-1.0)

            def A2c(it):
                for c in range(2):
                    TT(dve, ib[c][:, :], ib[c][:, :], xc[c][:, :], ALU.mult, [Bi[c], Bc[c]], [Bi[c]])

            def A3(it):
                half, ug = it
                ns = [ug * 2, ug * 2 + 1]
                for c in range(2):
                    n = ns[c]
                    OP(dve, (lambda h, c=c, n=n: h.tensor_tensor_scan(out=hsB[c][:, :], data0=ra[c][:, :], data1=ib[c][:, :],
                                                                   initial=hlast[:, n:n + 1], op0=ALU.mult, op1=ALU.add)),
                       reads=[Br[c], Bi[c], B_["hlast"]], writes=[Bh[c]])
                    ACT(hlast[:, n:n + 1], hsB[c][:, TH - 1:TH], AF.Copy, [Bh[c]], [B_["hlast"]])

            def gitems():
                return [(tch, c) for tch in range(NTH) for c in range(2)]

            def B1a(it):
                half, ug = it
                t0 = half * TH
                c0 = ug * 256
                pieces = [((lambda T, j=j, c0=c0: T["lru_w_in"][j].rearrange("(k p) c -> p k c", p=128)[:, :, c0:c0 + 256]), 0, KD, 256)]
                wt, wb = w_get(pieces)
                wga = wview(wt, 0, KD, 256)
                for b_, (tch, c) in enumerate(gitems()):
                    cs = slice(t0 + tch * 512, t0 + (tch + 1) * 512)
                    for k in range(KD):
                        MM(banks[b_][:], wga[:, k, c * 128:(c + 1) * 128], hT[:, k, cs], k == 0, k == KD - 1,
                           [wb, hb[cs.start // 512]], [bb[b_]])
                for b_, (tch, c) in enumerate(gitems()):
                    ls = slice(tch * 512, (tch + 1) * 512)
                    ACT(ggB[c][:, ls], banks[b_][:], AF.Square, [bb[b_]], [Bg[c]], scale=0.21145921366322695)

            def B1b(it):
                for b_, (tch, c) in enumerate(gitems()):
                    ls = slice(tch * 512, (tch + 1) * 512)
                    STT(dve, ggB[c][:, ls], ggB[c][:, ls], 1.0, banks[b_][:], ALU.add, ALU.mult, [Bg[c], bb[b_]], [Bg[c]])
                for b_, (tch, c) in enumerate(gitems()):
                    ls = slice(tch * 512, (tch + 1) * 512)
                    ACT(ggB[c][:, ls], ggB[c][:, ls], AF.Sigmoid, [Bg[c]], [Bg[c]], scale=1.5957691216057308)
                for b_, (tch, c) in enumerate(gitems()):
                    ls = slice(tch * 512, (tch + 1) * 512)
                    TT(dve, ggB[c][:, ls], ggB[c][:, ls], banks[b_][:], ALU.mult, [Bg[c], bb[b_]], [Bg[c]])

            NUG = KR // 2

            def B2(it):
                half, ug = it
                yo = 2 * (ug % 2)
                for tch in range(NTH):
                    for c in range(2):
                        ls = slice(tch * 512, (tch + 1) * 512)
                        TT(dve, yT[:, yo + c, ls], ggB[c][:, ls], hsB[c][:, ls], ALU.mult, [Bg[c], Bh[c]], [B_["yT"]])

            def B3(it):
                half, ug = it
                if ug % 2 == 0 and ug + 1 < NUG:
                    return
                u0 = ug - 1 if ug % 2 == 1 else ug
                nk = 4 if ug % 2 == 1 else 2
                ko = 0 if ug % 2 == 1 else 2 * (ug % 2)
                c0 = u0 * 256
                pieces = [((lambda T, j=j, c0=c0, nk=nk: T["lru_w_out"][j][c0:c0 + nk * 128, :].rearrange("(k p) c -> p k c", p=128)), 0, nk, D)]
                wot, wob = w_get(pieces)
                wo = wview(wot, 0, nk, D)
                for o in range(KD):
                    for tch in range(NTH):
                        ls = slice(tch * 512, (tch + 1) * 512)
                        bi = 6 + (rot["ev"] % 2)
                        rot["ev"] += 1
                        for k in range(nk):
                            MM(banks[bi][:], wo[:, k, o * 128:(o + 1) * 128], yT[:, ko + k, ls], k == 0, k == nk - 1,
                               [wob, B_["yT"]], [bb[bi]])
                        resid_update(bi, o, half * NTH + tch, sub)

            n_it = len(its)
            first_chunk = A1(its[0], defer=True)

            def first_xb(tcn):
                if tcn < NTH:
                    first_chunk(tcn)
            sub_norm(sub, after_apply=first_xb)
            A2a(its[0])
            A2b(its[0])
            A2c(its[0])
            A3(its[0])
            if n_it > 1:
                A1(its[1])
            for i in range(n_it):
                B1a(its[i])
                if i + 1 < n_it:
                    A2a(its[i + 1])
                B1b(its[i])
                if i + 1 < n_it:
                    A2b(its[i + 1])
                B2(its[i])
                B3(its[i])
                if i + 2 < n_it:
                    A1(its[i + 2])
                if i + 1 < n_it:
                    A2c(its[i + 1])
                    A3(its[i + 1])
            phase_barrier()

    def gelu_tanh(gg, ggb, cs, bank, bankb):
        ACT(gg[:, cs], bank[:], AF.Square, [bankb], [ggb])
        TS(dve, gg[:, cs], gg[:, cs], 0.044715, 1.0, ALU.mult, ALU.add, [ggb], [ggb])
        TT(dve, gg[:, cs], gg[:, cs], bank[:], ALU.mult, [ggb, bankb], [ggb])
        ACT(gg[:, cs], gg[:, cs], AF.Sigmoid, [ggb], [ggb], scale=1.5957691216057308)
        TT(dve, gg[:, cs], gg[:, cs], bank[:], ALU.mult, [ggb, bankb], [ggb])

    def epilogue():
        odst = out_ap.rearrange("(k p) s -> p k s", p=128)
        with tl("onT0", [128, KD, 512], F32) as onT0, tl("onT1", [128, KD, 512], F32) as onT1:
            onTs = [onT0, onT1]
            onb = [Buf("onT0"), Buf("onT1")]
            osem = [DmaSem(P, "osem0"), DmaSem(P, "osem1")]
            last = {}
            if final:
                TS(dve, fgs[:, :], sm[:, cfg.o_fg:cfg.o_fg + KD], float(np.sqrt(D)), None, ALU.mult, None, [smb], [fgb])
                iv = norm_stats(0)
            for tcn in range(NT):
                cs = slice(tcn * 512, (tcn + 1) * 512)
                if final:
                    iv_next = norm_stats(tcn + 1) if tcn + 1 < NT else None
                    o_ = onTs[tcn % 2]
                    norm_apply(tcn, iv, fgs, None, lambda k, t_, o_=o_: o_[:, k, :], [onb[tcn % 2]] * NT, [fgb])
                    iv = iv_next
                    tk = OP(sp, (lambda h, o_=o_, cs=cs: [h.dma_start(out=odst[:, :, cs], in_=o_[:, :, :])]),
                            reads=[onb[tcn % 2]], dma_sem=osem[tcn % 2], ndma=1)
                else:
                    tk = OP(sp, (lambda h, cs=cs: [h.dma_start(out=odst[:, :, cs], in_=xT[:, :, cs])]),
                            reads=[xb[tcn]], dma_sem=osem[tcn % 2], ndma=1)
                last[tcn % 2] = tk
            if not planning:
                for tk in last.values():
                    P.wait_ticket(sp, tk)
                for tk in last.values():
                    P.wait_ticket(act, tk)

    prologue()
    layers_in = []
    for (l, s_) in stages:
        if l not in layers_in:
            layers_in.append(l)
    if layers_in:
        mod_enqueue(layers_in[0])
    for (l, s_) in stages:
        cur["l"] = l
        nxt = [m for m in layers_in if m > l]
        cur["next"] = nxt[0] if nxt else None
        if s_ == 0:
            ffn_phase(l, 0, 0)
        elif s_ == 2:
            ffn_phase(l, 1, 2)
        else:
            if cur["next"] is not None:
                mod_enqueue(cur["next"])
            if l % 2 == 0:
                attn_phase(l, l // 2)
            else:
                lru_phase(l, l // 2)
    epilogue()
    if planning:
        return units
    assert wstate["consumed"] == len(units), (wstate, len(units))
    P.emit()
    return nc, P


def build_program(cfg, stages=None, final=True):
    plan = build(cfg, stages, final, plan=None)
    return build(cfg, stages, final, plan=plan)


_W_NAMES = ["mod_w", "ffn_w_gu", "ffn_w_down", "sb_w_qkv", "sb_w_o", "lru_w_in", "lru_w_r", "lru_w_i", "lru_w_out"]


def make_in_maps(cfg, inputs, B):
    consts = host_consts()
    shared = {n: np.ascontiguousarray(np.asarray(inputs[n], np.float32)) for n in _W_NAMES}
    in_maps = []
    for b in range(B):
        m = {"x": np.ascontiguousarray(np.asarray(inputs["x"][b], np.float32).T),
             "smalls": host_smalls(cfg, b, inputs), "consts": consts}
        m.update(shared)
        in_maps.append(m)
    return in_maps


_CACHE = {}


def kernel(**inputs):
    x = np.asarray(inputs["x"])
    B, S, D = x.shape
    cfg = Cfg(S=S, D=D)
    key = (S, D)
    if key not in _CACHE:
        _CACHE[key] = build_program(cfg)
    nc, _ = _CACHE[key]
    in_maps = make_in_maps(cfg, inputs, B)
    res = run_bass_kernel_spmd(nc, in_maps, core_ids=list(range(B)))
    out = np.stack([np.ascontiguousarray(np.asarray(r["out"], np.float32).T) for r in res.results], axis=0)
    return out
```
